# Optimizing a Trainium2 kernel written in Bass

```python
import jax
import jax.numpy as jnp
from jax import lax
import numpy as np

D_MODEL = 1024
BATCH = 8
SEQ = 4096
DEPTH = 2

GRID_W = 64
CTX_LEN = 256
EPS = 1e-6
CONV_DIM = 512
CONV_K = 31
MLSTM_HEADS = 4
MLSTM_DIM = 1024
MLSTM_HEAD_DIM = MLSTM_DIM // MLSTM_HEADS
MLSTM_CHUNK = 64
M_INIT = -1e30
ATTN_HEADS = 8
ATTN_KV_HEADS = 2
ATTN_GROUP = ATTN_HEADS // ATTN_KV_HEADS
ATTN_HEAD_DIM = 64
ATTN_DIM = ATTN_HEADS * ATTN_HEAD_DIM
KV_DIM = ATTN_KV_HEADS * ATTN_HEAD_DIM
Q_BLOCK = 128
ROPE_THETA = 10000.0
D_FF = 2816
FFN_CONV_K = 3
N_BRANCH = 3
IN_SIZES = (2 * CONV_DIM, 3 * MLSTM_DIM, MLSTM_DIM, 4 * MLSTM_HEADS, ATTN_DIM, KV_DIM, KV_DIM, N_BRANCH * D_MODEL)
N_IN = sum(IN_SIZES)

kernel_name = "hybrid_conv_mlstm_gqa_prefix_dit_block"


def rmsnorm(x, w):
    xf = x.astype(jnp.float32)
    y = xf * lax.rsqrt(jnp.mean(xf * xf, axis=-1, keepdims=True) + EPS)
    return (y * w.astype(jnp.float32)).astype(x.dtype)


def layernorm(x, w, b):
    xf = x.astype(jnp.float32)
    xc = xf - jnp.mean(xf, axis=-1, keepdims=True)
    y = xc * lax.rsqrt(jnp.mean(xc * xc, axis=-1, keepdims=True) + EPS)
    return (y * w.astype(jnp.float32) + b.astype(jnp.float32)).astype(x.dtype)


def dwconv(x, w, b):
    k = w.shape[0]
    y = lax.conv_general_dilated(x, w[:, None, :].astype(x.dtype), window_strides=(1,),
                                 padding=((k // 2, k // 2),), dimension_numbers=("NWC", "WIO", "NWC"),
                                 feature_group_count=x.shape[-1])
    return y + b.astype(x.dtype)


def axial_rope_tables(rows):
    row = jnp.broadcast_to(jnp.arange(rows, dtype=jnp.float32)[:, None], (rows, GRID_W)).reshape(-1)
    col = jnp.broadcast_to(jnp.arange(GRID_W, dtype=jnp.float32)[None, :], (rows, GRID_W)).reshape(-1)
    n_freq = ATTN_HEAD_DIM // 4
    inv_freq = ROPE_THETA ** (-jnp.arange(n_freq, dtype=jnp.float32) / n_freq)
    ang = jnp.concatenate([row[:, None] * inv_freq, col[:, None] * inv_freq], axis=-1)
    return jnp.cos(ang), jnp.sin(ang)


def apply_rope(x, cos, sin):
    half = x.shape[-1] // 2
    x1, x2 = x[..., :half], x[..., half:]
    c = cos[None, :, None, :]
    s = sin[None, :, None, :]
    return jnp.concatenate([x1 * c - x2 * s, x1 * s + x2 * c], axis=-1).astype(x.dtype)


def mlstm_chunked(q, k, v, log_i, log_f, state):
    b, h, n_tok, d = q.shape
    n_chunks = n_tok // MLSTM_CHUNK

    def to_chunks(a):
        return jnp.moveaxis(a.reshape(b, h, n_chunks, MLSTM_CHUNK, *a.shape[3:]), 2, 0)

    xs = (to_chunks(q), to_chunks(k), to_chunks(v), to_chunks(log_i), to_chunks(log_f))
    lower = jnp.tril(jnp.ones((MLSTM_CHUNK, MLSTM_CHUNK), dtype=bool))

    def step(carry, inp):
        c_mat, n_vec, m = carry
        qc, kc, vc, li, lf = inp
        cum = jnp.cumsum(lf, axis=-1)
        d_mat = jnp.where(lower, cum[..., :, None] - cum[..., None, :] + li[..., None, :], -jnp.inf)
        inter = cum + m[..., None]
        m_t = jnp.maximum(inter, jnp.max(d_mat, axis=-1))
        w_intra = jnp.exp(d_mat - m_t[..., None])
        w_inter = jnp.exp(inter - m_t)
        s = jnp.einsum("bhtd,bhsd->bhts", qc, kc) * w_intra
        num = w_inter[..., None] * jnp.einsum("bhtd,bhde->bhte", qc, c_mat) + jnp.einsum("bhts,bhse->bhte", s, vc)
        den = w_inter * jnp.einsum("bhtd,bhd->bht", qc, n_vec) + jnp.sum(s, axis=-1)
        h_out = num / jnp.maximum(jnp.abs(den), jnp.exp(-m_t))[..., None]
        cum_last = cum[..., -1]
        g = cum_last[..., None] - cum + li
        m_new = jnp.maximum(cum_last + m, jnp.max(g, axis=-1))
        decay = jnp.exp(cum_last + m - m_new)
        wk = jnp.exp(g - m_new[..., None])[..., None] * kc
        c_new = decay[..., None, None] * c_mat + jnp.einsum("bhsd,bhse->bhde", wk, vc)
        n_new = decay[..., None] * n_vec + jnp.sum(wk, axis=-2)
        return (c_new, n_new, m_new), h_out

    state, hs = lax.scan(step, state, xs)
    return jnp.moveaxis(hs, 0, 2).reshape(b, h, n_tok, d), state


def mlstm_prepare(qkv, gate_pre, gate_b):
    b, n, _ = qkv.shape
    q, k, v = jnp.split(qkv.astype(jnp.float32), 3, axis=-1)

    def to_heads(a):
        return a.reshape(b, n, MLSTM_HEADS, MLSTM_HEAD_DIM).transpose(0, 2, 1, 3)

    pre = (gate_pre.astype(jnp.float32) + gate_b.astype(jnp.float32)).reshape(b, n, 4, MLSTM_HEADS).transpose(2, 0, 3, 1)
    return to_heads(q), to_heads(k) * MLSTM_HEAD_DIM ** -0.5, to_heads(v), pre


def mlstm_bidirectional(qx, kx, vx, pre_x, qc, kc, vc, pre_c):
    b = qx.shape[0]
    zero = (jnp.zeros((b, MLSTM_HEADS, MLSTM_HEAD_DIM, MLSTM_HEAD_DIM), jnp.float32),
            jnp.zeros((b, MLSTM_HEADS, MLSTM_HEAD_DIM), jnp.float32),
            jnp.full((b, MLSTM_HEADS), M_INIT, jnp.float32))

    def flip(a):
        return jnp.flip(a, axis=2)

    lsig = jax.nn.log_sigmoid
    hc_f, st_f = mlstm_chunked(qc, kc, vc, pre_c[0], lsig(pre_c[1]), zero)
    hx_f, _ = mlstm_chunked(qx, kx, vx, pre_x[0], lsig(pre_x[1]), st_f)
    hc_b, st_b = mlstm_chunked(flip(qc), flip(kc), flip(vc), flip(pre_c[2]), flip(lsig(pre_c[3])), zero)
    hx_b, _ = mlstm_chunked(flip(qx), flip(kx), flip(vx), flip(pre_x[2]), flip(lsig(pre_x[3])), st_b)
    return hx_f + flip(hx_b), hc_f + flip(hc_b)


def mlstm_readout(h, o_pre, norm_w, w_o):
    b, _, n, _ = h.shape
    hn = rmsnorm(h.transpose(0, 2, 1, 3), norm_w.reshape(MLSTM_HEADS, MLSTM_HEAD_DIM)).reshape(b, n, MLSTM_DIM)
    return (hn.astype(o_pre.dtype) * jax.nn.sigmoid(o_pre)) @ w_o


def attn_heads(a, n_heads, norm_w):
    b, n, _ = a.shape
    return rmsnorm(a.reshape(b, n, n_heads, ATTN_HEAD_DIM), norm_w)


def attend(q, k, v):
    s = jnp.einsum("bqhgd,bkhd->bhgqk", q, k).astype(jnp.float32) * ATTN_HEAD_DIM ** -0.5
    p = jax.nn.softmax(s, axis=-1).astype(v.dtype)
    return jnp.einsum("bhgqk,bkhd->bqhgd", p, v)


def latent_attention(q, k, v):
    b, n = q.shape[:2]
    n_blocks = n // Q_BLOCK
    qb = jnp.moveaxis(q.reshape(b, n_blocks, Q_BLOCK, ATTN_KV_HEADS, ATTN_GROUP, ATTN_HEAD_DIM), 1, 0)
    o = lax.map(lambda qi: attend(qi, k, v), qb)
    return jnp.moveaxis(o, 0, 1).reshape(b, n, ATTN_DIM)


def conformer_conv(u, dw_w, dw_b, ln_w, ln_b, w_o):
    a, g = jnp.split(u, 2, axis=-1)
    h = dwconv(a * jax.nn.sigmoid(g), dw_w, dw_b)
    return jax.nn.silu(layernorm(h, ln_w, ln_b)) @ w_o


def conv_ffn(h, w_up, dw_w, dw_b, w_down):
    u = dwconv(h @ w_up, dw_w, dw_b)
    a, g = jnp.split(u, 2, axis=-1)
    return (a * jax.nn.silu(g)) @ w_down


def token_mixers(hx, hc, cos, sin, want_ctx, w_in, branch_gate_b, conv_dw_w, conv_dw_b, conv_ln_w, conv_ln_b,
                 w_conv_out, mlstm_gate_b, mlstm_norm_w, w_mlstm_out, q_norm_w, k_norm_w, w_attn_out, w_out):
    offsets = [int(o) for o in np.cumsum(IN_SIZES)[:-1]]
    cu_x, mqkv_x, mo_x, mg_x, aq_x, ak_x, av_x, bg_x = jnp.split(hx @ w_in, offsets, axis=-1)
    cu_c, mqkv_c, mo_c, mg_c, aq_c, ak_c, av_c, bg_c = jnp.split(hc @ w_in, offsets, axis=-1)
    b, n, _ = hx.shape
    n_ctx = hc.shape[1]

    conv_x = conformer_conv(cu_x, conv_dw_w, conv_dw_b, conv_ln_w, conv_ln_b, w_conv_out)

    h_x, h_c = mlstm_bidirectional(*mlstm_prepare(mqkv_x, mg_x, mlstm_gate_b), *mlstm_prepare(mqkv_c, mg_c, mlstm_gate_b))
    mlstm_x = mlstm_readout(h_x, mo_x, mlstm_norm_w, w_mlstm_out)

    k_c = attn_heads(ak_c, ATTN_KV_HEADS, k_norm_w)
    v_c = av_c.reshape(b, n_ctx, ATTN_KV_HEADS, ATTN_HEAD_DIM)
    k_x = apply_rope(attn_heads(ak_x, ATTN_KV_HEADS, k_norm_w), cos, sin)
    v_x = av_x.reshape(b, n, ATTN_KV_HEADS, ATTN_HEAD_DIM)
    q_x = apply_rope(attn_heads(aq_x, ATTN_HEADS, q_norm_w), cos, sin).reshape(b, n, ATTN_KV_HEADS, ATTN_GROUP, ATTN_HEAD_DIM)
    attn_x = latent_attention(q_x, jnp.concatenate([k_c, k_x], axis=1), jnp.concatenate([v_c, v_x], axis=1)) @ w_attn_out

    def merge(conv_o, mlstm_o, attn_o, gate_pre):
        g_conv, g_mlstm, g_attn = jnp.split(jax.nn.sigmoid(gate_pre + branch_gate_b), N_BRANCH, axis=-1)
        return (g_conv * conv_o + g_mlstm * mlstm_o + g_attn * attn_o) @ w_out

    out_x = merge(conv_x, mlstm_x, attn_x, bg_x)
    if not want_ctx:
        return out_x, None
    conv_c = conformer_conv(cu_c, conv_dw_w, conv_dw_b, conv_ln_w, conv_ln_b, w_conv_out)
    mlstm_c = mlstm_readout(h_c, mo_c, mlstm_norm_w, w_mlstm_out)
    q_c = attn_heads(aq_c, ATTN_HEADS, q_norm_w).reshape(b, n_ctx, ATTN_KV_HEADS, ATTN_GROUP, ATTN_HEAD_DIM)
    attn_c = attend(q_c, k_c, v_c).reshape(b, n_ctx, ATTN_DIM) @ w_attn_out
    return out_x, merge(conv_c, mlstm_c, attn_c, bg_c)


def setup_inputs(seed: int = 0) -> dict:
    key = jax.random.key(seed)
    ks = jax.random.split(key, 32)
    f32 = jnp.float32
    D = D_MODEL
    L = DEPTH

    def nrm(i, shape, scale):
        return jax.random.normal(ks[i], shape, f32) * scale

    fb = jnp.linspace(3.0, 6.0, MLSTM_HEADS, dtype=f32)
    zh = jnp.zeros((MLSTM_HEADS,), f32)
    gate_base = jnp.concatenate([zh, fb, zh, fb])
    return {
        "x": nrm(0, (BATCH, SEQ, D), 1.0),
        "c": nrm(1, (BATCH, D), 1.0),
        "ctx": nrm(2, (BATCH, CTX_LEN, D), 1.0),
        "c_ctx": nrm(3, (D,), 1.0),
        "ada_w": nrm(4, (L, D, 6 * D), 0.5 * D ** -0.5),
        "ada_b": nrm(5, (L, 6 * D), 0.02),
        "norm1_w": 1.0 + nrm(6, (L, D), 0.02),
        "norm2_w": 1.0 + nrm(7, (L, D), 0.02),
        "w_in": nrm(8, (L, D, N_IN), D ** -0.5),
        "branch_gate_b": nrm(9, (L, N_BRANCH * D), 0.02),
        "conv_dw_w": nrm(10, (L, CONV_K, CONV_DIM), CONV_K ** -0.5),
        "conv_dw_b": nrm(11, (L, CONV_DIM), 0.02),
        "conv_ln_w": 1.0 + nrm(12, (L, CONV_DIM), 0.02),
        "conv_ln_b": nrm(13, (L, CONV_DIM), 0.02),
        "w_conv_out": nrm(14, (L, CONV_DIM, D), CONV_DIM ** -0.5),
        "mlstm_gate_b": gate_base + nrm(15, (L, 4 * MLSTM_HEADS), 0.1),
        "mlstm_norm_w": 1.0 + nrm(16, (L, MLSTM_DIM), 0.02),
        "w_mlstm_out": nrm(17, (L, MLSTM_DIM, D), MLSTM_DIM ** -0.5),
        "q_norm_w": 1.0 + nrm(18, (L, ATTN_HEAD_DIM), 0.02),
        "k_norm_w": 1.0 + nrm(19, (L, ATTN_HEAD_DIM), 0.02),
        "w_attn_out": nrm(20, (L, ATTN_DIM, D), ATTN_DIM ** -0.5),
        "w_out": nrm(21, (L, D, D), D ** -0.5),
        "ffn_w_up": nrm(22, (L, D, 2 * D_FF), D ** -0.5),
        "ffn_conv_w": nrm(23, (L, FFN_CONV_K, 2 * D_FF), FFN_CONV_K ** -0.5),
        "ffn_conv_b": nrm(24, (L, 2 * D_FF), 0.02),
        "ffn_w_down": nrm(25, (L, D_FF, D), D_FF ** -0.5),
    }


def reference(x, c, ctx, c_ctx, ada_w, ada_b, norm1_w, norm2_w, w_in, branch_gate_b, conv_dw_w, conv_dw_b,
              conv_ln_w, conv_ln_b, w_conv_out, mlstm_gate_b, mlstm_norm_w, w_mlstm_out, q_norm_w, k_norm_w,
              w_attn_out, w_out, ffn_w_up, ffn_conv_w, ffn_conv_b, ffn_w_down):
    rows = x.shape[1] // GRID_W
    cos, sin = axial_rope_tables(rows)
    xc = ctx
    for l in range(DEPTH):
        want_ctx = l < DEPTH - 1
        mod_x = jax.nn.silu(c) @ ada_w[l] + ada_b[l]
        mod_c = jax.nn.silu(c_ctx) @ ada_w[l] + ada_b[l]
        sh1, sc1, g1, sh2, sc2, g2 = jnp.split(mod_x[:, None, :], 6, axis=-1)
        sh1c, sc1c, g1c, sh2c, sc2c, g2c = jnp.split(mod_c, 6, axis=-1)

        hx = rmsnorm(x, norm1_w[l]) * (1.0 + sc1) + sh1
        hc = rmsnorm(xc, norm1_w[l]) * (1.0 + sc1c) + sh1c
        mix_x, mix_c = token_mixers(hx, hc, cos, sin, want_ctx, w_in[l], branch_gate_b[l], conv_dw_w[l],
                                    conv_dw_b[l], conv_ln_w[l], conv_ln_b[l], w_conv_out[l], mlstm_gate_b[l],
                                    mlstm_norm_w[l], w_mlstm_out[l], q_norm_w[l], k_norm_w[l], w_attn_out[l], w_out[l])
        x = x + g1 * mix_x
        hx = rmsnorm(x, norm2_w[l]) * (1.0 + sc2) + sh2
        x = x + g2 * conv_ffn(hx, ffn_w_up[l], ffn_conv_w[l], ffn_conv_b[l], ffn_w_down[l])
        if want_ctx:
            xc = xc + g1c * mix_c
            hc = rmsnorm(xc, norm2_w[l]) * (1.0 + sc2c) + sh2c
            xc = xc + g2c * conv_ffn(hc, ffn_w_up[l], ffn_conv_w[l], ffn_conv_b[l], ffn_w_down[l])
    return x
```

```python
import numpy as np
import ml_dtypes
import concourse.bass as bass
import concourse.mybir as mybir
from concourse.bass_utils import run_bass_kernel_spmd
from contextlib import ExitStack

F32 = mybir.dt.float32
BF16 = mybir.dt.bfloat16
ALU = mybir.AluOpType
AF = mybir.ActivationFunctionType
AX = mybir.AxisListType

D = 1024
NTOK = 4352
NT = 34
NCTX = 256
DEPTH = 2
NIN = 8976
DFF = 2816
EPS = 1e-6
COMPUTE = ("pe", "act", "dve", "pool")
SAME_ENG_SYNC = True
DEFER_STORES = True
SAME_ENG_RAW_ONLY = False
_CACHE = {}


class Op:
    __slots__ = ("id", "eng", "emit", "dma", "deps", "signal", "token")

    def __init__(self, id, eng, emit, dma):
        self.id = id; self.eng = eng; self.emit = emit; self.dma = dma
        self.deps = set(); self.signal = False; self.token = None


class Sched:
    def __init__(self, nc, es):
        self.nc = nc
        self.es = es
        self.ops = []
        self.eng_ops = {e: [] for e in ("pe", "act", "dve", "pool", "sp")}
        self.last_w = {}
        self.readers = {}
        self.n_dma_sems = {"sp": 40}
        self.dma_since_bar = []
        self.pending_stores = []

    def add(self, eng, emit, reads=(), writes=(), dma=False, extra_deps=()):
        op = Op(len(self.ops), eng, emit, dma)
        deps = set(extra_deps)
        same_ok = (not dma) and SAME_ENG_RAW_ONLY
        for b in reads:
            w = self.last_w.get(b)
            if w is not None:
                deps.add(w)
            if b.startswith("ps"):
                for r_ in self.readers.get(b, ()):
                    if self.ops[r_].eng != eng:
                        deps.add(r_)
        for b in writes:
            w = self.last_w.get(b)
            if w is not None and not (same_ok and self.ops[w].eng == eng and not self.ops[w].dma):
                deps.add(w)
            for r_ in self.readers.get(b, ()):
                if same_ok and self.ops[r_].eng == eng and not self.ops[r_].dma:
                    continue
                deps.add(r_)
        for b in reads:
            lst = self.readers.setdefault(b, [])
            if not dma:
                lst[:] = [r for r in lst if self.ops[r].dma or self.ops[r].eng != eng]
            lst.append(op.id)
        for b in writes:
            self.last_w[b] = op.id
            self.readers[b] = []
        deps.discard(op.id)
        best = {}
        out = set()
        for d in deps:
            o = self.ops[d]
            if o.dma:
                out.add(d)
            else:
                if o.eng == eng and not dma and (eng == "pe" or not SAME_ENG_SYNC):
                    continue
                if o.eng not in best or best[o.eng] < d:
                    best[o.eng] = d
        out.update(best.values())
        op.deps = out
        self.ops.append(op)
        self.eng_ops[eng].append(op)
        if dma:
            self.dma_since_bar.append(op.id)
        return op

    def barrier(self):
        self.flush_stores()
        ids = list(self.dma_since_bar)
        for e in COMPUTE:
            if self.eng_ops[e]:
                for o in reversed(self.eng_ops[e]):
                    if o.emit is not None and not o.dma:
                        ids.append(o.id)
                        break
        self.dma_since_bar = []
        for e in ("pe", "act", "dve", "pool", "sp"):
            self.add(e, None, extra_deps=ids)
        self.last_w = {}
        self.readers = {}

    def dma(self, out, in_, r=(), w=(), q="sp"):
        op = self.add(q, lambda e: e.dma_start(out=out, in_=in_), r, w, dma=True)
        if DEFER_STORES and self.pending_stores:
            self.flush_stores()
        return op

    def store(self, out, in_, r=(), w=()):
        if not DEFER_STORES:
            return self.dma(out, in_, r, w)
        if self.pending_stores:
            self.flush_stores()
        snap = {b: self.last_w.get(b) for b in r}
        self.pending_stores.append((out, in_, tuple(r), tuple(w), snap))

    def flush_stores(self):
        pend, self.pending_stores = self.pending_stores, []
        for (out, in_, r, w, snap) in pend:
            for b in r:
                assert self.last_w.get(b) == snap[b], ("deferred store source overwritten before flush", b)
            self.add("sp", lambda e, out=out, in_=in_: e.dma_start(out=out, in_=in_), r, w, dma=True)

    def mm(self, out, lhsT, rhs, start, stop, r, w):
        return self.add("pe", lambda e: e.matmul(out, lhsT=lhsT, rhs=rhs, start=start, stop=stop), r, w)

    def act(self, out, in_, func, r, w, bias=None, scale=None, accum=None):
        kw = {}
        if bias is not None:
            kw["bias"] = bias
        if scale is not None:
            kw["scale"] = scale
        if accum is not None:
            kw["accum_out"] = accum
        return self.add("act", lambda e: e.activation(out, in_, func, **kw), r, w)

    def tt(self, eng, out, in0, in1, op, r, w):
        return self.add(eng, lambda e: e.tensor_tensor(out, in0, in1, op=op), r, w)

    def ts(self, eng, out, in0, s1, s2, op0, op1, r, w):
        if s2 is None:
            return self.add(eng, lambda e: e.tensor_scalar(out, in0, s1, None, op0=op0), r, w)
        return self.add(eng, lambda e: e.tensor_scalar(out, in0, s1, s2, op0=op0, op1=op1), r, w)

    def stt(self, eng, out, in0, scalar, in1, op0, op1, r, w):
        return self.add(eng, lambda e: e.scalar_tensor_tensor(out, in0, scalar, in1, op0=op0, op1=op1), r, w)

    def copy(self, eng, out, in_, r, w):
        if eng == "act":
            return self.add(eng, lambda e: e.copy(out, in_), r, w)
        return self.add(eng, lambda e: e.tensor_copy(out, in_), r, w)

    def memset(self, eng, ap, val, w):
        return self.add(eng, lambda e: e.memset(ap, val), (), w)

    def recip(self, out, in_, r, w):
        return self.add("dve", lambda e: e.reciprocal(out, in_), r, w)

    def reduce_sum(self, out, in_, r, w):
        return self.add("dve", lambda e: e.tensor_reduce(out, in_, axis=AX.X, op=ALU.add), r, w)

    def finalize(self, final_reads=()):
        nc = self.nc
        self.flush_stores()
        self.add("sp", None, reads=final_reads, writes=(), extra_deps=list(self.dma_since_bar))
        for op in self.ops:
            for d in op.deps:
                self.ops[d].signal = True
        sems = {}
        for e in COMPUTE:
            sems[e] = self.es.enter_context(nc.semaphore("sem_" + e))
        dsems = {}
        for q, n in self.n_dma_sems.items():
            dsems[q] = [self.es.enter_context(nc.semaphore(f"dsem_{q}_{i}")) for i in range(n)]
        cnt = {e: 0 for e in COMPUTE}
        dcnt = {q: [0] * n for q, n in self.n_dma_sems.items()}
        drr = {q: 0 for q in self.n_dma_sems}
        prev_on_sem = {}
        for e, lst in self.eng_ops.items():
            for op in lst:
                if op.dma:
                    j = drr[e]; drr[e] = (j + 1) % len(dsems[e])
                    prev = dcnt[e][j]
                    dcnt[e][j] += 16
                    op.token = (dsems[e][j], dcnt[e][j], ("d", e, j))
                    prev_on_sem[op.id] = (dsems[e][j], prev, ("d", e, j))
                elif op.signal:
                    cnt[e] += 1
                    op.token = (sems[e], cnt[e], ("c", e))
        self.max_counts = (dict(cnt), {q: max(v) for q, v in dcnt.items()})
        ops = self.ops

        def emit_engine(ename, eh):
            observed = {}
            for op in self.eng_ops[ename]:
                waits = []
                for d in sorted(op.deps):
                    waits.append(ops[d].token)
                if op.dma:
                    sem, val, key = prev_on_sem[op.id]
                    if val > 0:
                        waits.append((sem, val, key))
                for sem, val, key in waits:
                    if observed.get(key, 0) < val:
                        eh.wait_ge(sem, val)
                        observed[key] = val
                if op.emit is None:
                    continue
                ins = op.emit(eh)
                if op.dma:
                    ins.then_inc(op.token[0], 16)
                elif op.signal:
                    ins.then_inc(op.token[0], 1)

        with nc.Block() as block:
            @block.sync
            def _(e):
                emit_engine("sp", e)

            @block.tensor
            def _(e):
                emit_engine("pe", e)

            @block.scalar
            def _(e):
                emit_engine("act", e)

            @block.vector
            def _(e):
                emit_engine("dve", e)

            @block.gpsimd
            def _(e):
                emit_engine("pool", e)


class Arena:
    def __init__(self, t, nwords):
        self.t = t; self.n = nwords; self.off = 0; self.gen = 0

    def reset(self):
        self.off = 0; self.gen += 1

    def f32(self, n):
        assert self.off + n <= self.n, ("arena overflow", self.off + n, self.n)
        v = self.t[:, self.off:self.off + n]
        self.off += n
        return v

    def bf16(self, n):
        words = (n + 1) // 2
        assert self.off + words <= self.n, ("arena overflow", self.off + words, self.n)
        v = self.t[:, self.off:self.off + words].bitcast(BF16)
        self.off += words
        return v[:, 0:n]


def tokblocks():
    return [(0, 256)] + [(256 + 512 * i, 512) for i in range(8)]


PAD31 = 15


def padpos(tok, pad):
    return pad + tok if tok < NCTX else 2 * pad + tok


def build(debug=False, nlayers=DEPTH, stop_phase=99):
    nc = bass.Bass("TRN2", target_bir_lowering=False)
    kin = "ExternalInput"

    def din(name, shape, dt=F32):
        return nc.dram_tensor(name, list(shape), dt, kind=kin).ap()

    dbg_kind = "ExternalOutput" if debug else "Internal"

    def dscr(name, shape, dt=F32):
        return nc.dram_tensor(name, list(shape), dt, kind=dbg_kind).ap()

    xin = din("xin", [NTOK, D])
    cT = din("cT", [128, 16])
    ada_w = din("ada_w", [DEPTH, D, 6 * D])
    ada_b = din("ada_b", [DEPTH, 6 * D])
    norm1_w = din("norm1_w", [DEPTH, D])
    norm2_w = din("norm2_w", [DEPTH, D])
    w_in = din("w_in", [DEPTH, D, NIN])
    bgb = din("bgb", [DEPTH, 128, 24])
    convw = din("convw", [DEPTH, 128, 4 * 31])
    convb = din("convb", [DEPTH, 128, 4])
    lnw = din("lnw", [DEPTH, 128, 4])
    lnb = din("lnb", [DEPTH, 128, 4])
    w_conv_out = din("w_conv_out", [DEPTH, 512, D])
    gate_b = din("mlstm_gate_b", [DEPTH, 16])
    mnorm_w = din("mlstm_norm_w", [DEPTH, D])
    w_mlstm_out = din("w_mlstm_out", [DEPTH, D, D])
    qnw = din("q_norm_w", [DEPTH, 64])
    knw = din("k_norm_w", [DEPTH, 64])
    w_attn_out = din("w_attn_out", [DEPTH, 512, D])
    w_out = din("w_out", [DEPTH, D, D])
    w_up = din("ffn_w_up", [DEPTH, D, 2 * DFF])
    fcw = din("fcw", [DEPTH, 128, 44 * 3])
    fcb = din("fcb", [DEPTH, 128, 44])
    w_down = din("ffn_w_down", [DEPTH, DFF, D])
    ident_d = din("ident", [128, 128])
    triU_d = din("triU", [128, 128])
    triL_d = din("triL", [128, 128])
    shsel_d = din("shsel", [128, 64])
    rope_d = din("rope", [NTOK, 64])
    out_d = nc.dram_tensor("out", [4096, D], F32, kind="ExternalOutput").ap()

    xres1 = dscr("xres1", [NTOK, D])
    xres2 = dscr("xres2", [NTOK, D])
    convT = dscr("convT", [512, NTOK])
    mQT = dscr("mQT", [1024, NTOK], BF16)
    mKT = dscr("mKT", [1024, NTOK], BF16)
    mK = dscr("mK", [NTOK, 1024], BF16)
    mV = dscr("mV", [NTOK, 1024], BF16)
    mO = dscr("mO", [NTOK, 1024], BF16)
    aQT = dscr("aQT", [512, NTOK], BF16)
    aKT = dscr("aKT", [128, NTOK], BF16)
    aV = dscr("aV", [NTOK, 128], BF16)
    bgT = dscr("bgT", [3072, NTOK], BF16)
    hF = dscr("hF", [NTOK, 1024])
    hmT = dscr("hmT", [1024, NTOK], BF16)
    hcT = dscr("hcT", [512, NTOK], BF16)
    aoT = dscr("aoT", [512, NTOK], BF16)
    mergedT = dscr("mergedT", [1024, NTOK], BF16)
    prodT = dscr("prodT", [DFF, NTOK], BF16)
    dbg_hT = dscr("dbg_hT", [128, 8 * NTOK], BF16) if debug else None
    dbg_gt = dscr("dbg_gt", [128, NT * 16]) if debug else None
    dbg_gates = dscr("dbg_gates", [128, 8 * NT * 4]) if debug else None

    with ExitStack() as es:
        S = Sched(nc, es)
        sbt = lambda n, s, d=F32: es.enter_context(nc.sbuf_tensor("sb_" + n, s, d))
        hT_t = sbt("hT", [128, 8 * NTOK], BF16)
        hT = hT_t[:].rearrange("p (k t) -> p k t", k=8)
        modd = nc.dram_tensor("modd", [128, 2 * 6 * D], F32, kind=dbg_kind).ap()
        moddv = modd.rearrange("p (r c f) -> p r c f", r=2, c=6)
        ident = sbt("identf", [128, 128]); identb = sbt("identb", [128, 128], BF16)
        triU = sbt("triU", [128, 128]); triL = sbt("triL", [128, 128])
        ones = sbt("ones", [128, 128]); shsel = sbt("shsel", [128, 64])
        Gt_t = sbt("Gt", [128, NT * 16])
        Gt = Gt_t[:].rearrange("p (t g) -> p t g", g=16)
        ARN = 34000
        arena_t = sbt("arena", [128, ARN])
        A = Arena(arena_t, ARN)
        PS = [es.enter_context(nc.psum_tensor(f"ps{i}", [128, 512], F32)) for i in range(8)]

        def P(i):
            return PS[i][:], f"ps{i}"

        S.dma(ident[:], ident_d, w=["ident"])
        S.dma(triU[:], triU_d, w=["triU"])
        S.dma(triL[:], triL_d, w=["triL"])
        S.dma(shsel[:], shsel_d, w=["shsel"])
        S.memset("pool", ones[:], 1.0, w=["ones"])
        eps_t = sbt("eps_t", [128, 1])
        S.memset("pool", eps_t[:], EPS, w=["eps_t"])
        S.copy("dve", identb[:], ident[:], r=["ident"], w=["identb"])

        def tiny_rstd(ss_ap, n, cnt, tag, rdeps):
            S.act(ss_ap, ss_ap, AF.Sqrt, r=rdeps, w=[tag], scale=1.0 / cnt, bias=eps_t[:, 0:1])
            S.recip(ss_ap, ss_ap, r=[tag], w=[tag])

        def norm_tile(xt, xname, tt, modl, bufs, it):
            r = 1 if tt < 2 else 0
            junk, ss, t1, hx = bufs
            k = it % 2
            S.act(junk, xt, AF.Square, r=[xname], w=["nt_junk", f"nt_ss{k}"], accum=ss[k])
            tiny_rstd(ss[k], 1, D, f"nt_ss{k}", [f"nt_ss{k}"])
            S.stt("dve", t1[k], xt, ss[k], modl[:, r, 1, :], ALU.mult, ALU.mult, r=[xname, f"nt_ss{k}", "modl"], w=[f"nt_t1{k}"])
            S.tt("dve", hx[k], t1[k], modl[:, r, 0, :], ALU.add, r=[f"nt_t1{k}", "modl"], w=[f"nt_hx{k}"])
            for half in range(2):
                pa, pn = P(half + 2 * k)
                for q in range(4):
                    kc = half * 4 + q
                    S.mm(pa[:, q * 128:(q + 1) * 128], hx[k][:, kc * 128:(kc + 1) * 128], identb[:], True, True,
                         r=[f"nt_hx{k}", "identb"], w=[pn])
                dst = hT[:, half * 4:(half + 1) * 4, tt * 128:(tt + 1) * 128]
                src = pa.rearrange("p (q t) -> p q t", q=4)
                S.copy("act" if half == 0 else "dve", dst, src, r=[pn], w=[f"hT{tt}"])

        def norm_bufs():
            junk = A.f32(D)
            ss = [A.f32(1), A.f32(1)]
            t1 = [A.f32(D), A.f32(D)]
            hx = [A.bf16(D), A.bf16(D)]
            return junk, ss, t1, hx

        hT_all = [f"hT{tt}" for tt in range(NT)]

        def load_weight(dst_bf, src_ap, stage, sname, dname, eng, nk, ncols):
            S.dma(stage, src_ap, w=[sname])
            S.copy(eng, dst_bf, stage, r=[sname], w=[dname])

        for l in range(nlayers):
            xsrc = xin if l == 0 else xres2
            last = (l == DEPTH - 1)
            A.reset()
            mod_f = A.f32(2 * 6 * D)
            mod = mod_f.rearrange("p (r c f) -> p r c f", r=2, c=6)
            ct = A.f32(16); sc = A.f32(16); scb = A.f32(16 * 128)
            scbv = scb.rearrange("p (r k m) -> p r k m", r=2, k=8)
            S.dma(ct, cT, w=["ct"])
            S.act(sc, ct, AF.Silu, r=["ct"], w=["sc"])
            S.copy("dve", scbv, sc.rearrange("p (r k) -> p r k", r=2).unsqueeze(3).broadcast_to([128, 2, 8, 128]), r=["sc"], w=["scb"])
            aw = [A.f32(8 * 512), A.f32(8 * 512)]
            ab = [A.f32(512), A.f32(512)]
            for cb in range(12):
                k = cb % 2
                S.dma(aw[k].rearrange("p (k c) -> p k c", k=8),
                      ada_w[l, :, cb * 512:(cb + 1) * 512].rearrange("(k p) c -> p k c", p=128), w=[f"aw{k}"])
                S.dma(ab[k], ada_b[l, cb * 512:(cb + 1) * 512].partition_broadcast(128), w=[f"ab{k}"])
                for r in range(2):
                    pa, pn = P(r + 2 * k)
                    for kc in range(8):
                        S.mm(pa, scbv[:, r, kc, :], aw[k][:, kc * 512:(kc + 1) * 512], kc == 0, kc == 7,
                             r=["scb", f"aw{k}"], w=[pn])
                    ci, off = divmod(cb * 512, D)
                    S.tt("dve", mod[:, r, ci, off:off + 512], pa, ab[k], ALU.add, r=[pn, f"ab{k}"], w=["mods"])
            nwb = A.f32(D)
            for (ci, nw) in ((1, norm1_w), (4, norm2_w)):
                S.dma(nwb, nw[l, :].partition_broadcast(128), w=["nwb"])
                for r in range(2):
                    S.stt("dve", mod[:, r, ci, :], mod[:, r, ci, :], 1.0, nwb, ALU.add, ALU.mult, r=["mods", "nwb"], w=["mods"])
            S.dma(modd, mod_f, r=["mods"], w=["modd"])
            S.barrier()
            if stop_phase <= 0:
                break

            A.reset()
            nb = norm_bufs()
            xt = [A.f32(D), A.f32(D)]
            modl = A.f32(4 * D).rearrange("p (r c f) -> p r c f", r=2, c=2)
            S.dma(modl, moddv[:, :, 0:2, :], w=["modl"])
            for tt in range(NT):
                k = tt % 2
                S.dma(xt[k], xsrc[tt * 128:(tt + 1) * 128, :], w=[f"xt{k}"])
                norm_tile(xt[k], f"xt{k}", tt, modl, nb, tt)
            if debug and l == 0:
                S.dma(dbg_hT, hT_t[:], r=hT_all, w=["dbg_hT"])
            S.barrier()
            if stop_phase <= 1:
                break

            A.reset()
            stage = A.f32(8 * 512)
            wb = [A.bf16(8 * 512), A.bf16(8 * 512)]
            PADTOT = NTOK + 3 * PAD31
            glu = A.bf16(PADTOT)
            sg = [A.f32(512), A.f32(512)]
            cst = [A.f32(512), A.f32(512)]
            cw = A.f32(4 * 31); cbias = A.f32(4)
            Dg = A.bf16(124 * 128)
            Dgv = Dg.rearrange("p (i m) -> p i m", i=124)
            S.dma(cw, convw[l], w=["cw"])
            S.dma(cbias, convb[l], w=["cbias"])
            glu_names = [f"glu{bi}" for bi in range(9)]
            S.memset("pool", glu, 0.0, w=glu_names)
            for i_ in range(124):
                S.ts("dve", Dgv[:, i_, :], ident[:], cw[:, i_:i_ + 1], None, ALU.mult, None,
                     r=["ident", "cw"], w=[f"Dg{i_ // 31}"])
            wcnt = 0

            def wload(c0, ncols, extra=()):
                nonlocal wcnt
                k = wcnt % 2
                wcnt += 1
                ranges = [(c0, ncols)] + list(extra)
                tot = sum(n for _, n in ranges)
                st = stage[:, 0:8 * tot].rearrange("p (k c) -> p k c", k=8)
                dstv = wb[k][:, 0:8 * tot].rearrange("p (k c) -> p k c", k=8)
                off = 0
                for (cc, n) in ranges:
                    S.dma(st[:, :, off:off + n], w_in[l, :, cc:cc + n].rearrange("(k p) c -> p k c", p=128), w=["wstage"])
                    off += n
                S.copy("dve" if k == 0 else "act", dstv, st, r=["wstage"], w=[f"wb{k}"])
                return dstv, f"wb{k}"

            pscnt = 0

            def ftype_block(wv, wn, tb0, ntok, col0=0):
                nonlocal pscnt
                pa, pn = P(pscnt % 6)
                pscnt += 1
                for kc in range(8):
                    S.mm(pa[:, 0:ntok], wv[:, kc, col0:col0 + 128], hT[:, kc, tb0:tb0 + ntok], kc == 0, kc == 7,
                         r=[wn] + hT_all[tb0 // 128:(tb0 + ntok) // 128], w=[pn])
                return pa[:, 0:ntok], pn

            nxt = (wload(0, 128), wload(512, 128))
            ccnt_ = 0
            for j in range(4):
                (wa, wan), (wg, wgn) = nxt
                for bi, (tb0, ntok) in enumerate(tokblocks()):
                    pg, pgn = ftype_block(wg, wgn, tb0, ntok)
                    k = bi % 2
                    S.act(sg[k][:, 0:ntok], pg, AF.Sigmoid, r=[pgn], w=[f"sg{k}"])
                    pa_, pan = ftype_block(wa, wan, tb0, ntok)
                    c0 = padpos(tb0, PAD31)
                    S.tt("dve", glu[:, c0:c0 + ntok], pa_, sg[k][:, 0:ntok], ALU.mult, r=[pan, f"sg{k}"], w=[f"glu{bi}"])
                if j + 1 < 4:
                    nxt = (wload((j + 1) * 128, 128), wload(512 + (j + 1) * 128, 128))
                for bi, (tb0, ntok) in enumerate(tokblocks()):
                    o0 = tb0 if tb0 < NCTX else tb0 + PAD31
                    pc_, pcn_ = P(6 + (ccnt_ % 2))
                    kk = ccnt_ % 2
                    ccnt_ += 1
                    for tap in range(31):
                        S.mm(pc_[:, 0:ntok], Dgv[:, j * 31 + tap, :], glu[:, o0 + tap:o0 + tap + ntok], tap == 0, tap == 30,
                             r=glu_names + [f"Dg{j}"], w=[pcn_])
                    S.act(cst[kk][:, 0:ntok], pc_[:, 0:ntok], AF.Identity, r=[pcn_, "cbias"], w=[f"cst{kk}"], bias=cbias[:, j:j + 1])
                    S.store(convT[j * 128:(j + 1) * 128, tb0:tb0 + ntok], cst[kk][:, 0:ntok], r=[f"cst{kk}"], w=[f"convT{j}_{bi}"])
            S.barrier()
            if stop_phase <= 2:
                break

            A.reset()
            stage = A.f32(8 * 512)
            wb = [A.bf16(8 * 512), A.bf16(8 * 512)]
            stg = [A.bf16(NTOK), A.bf16(NTOK)]
            bgbias = A.f32(24)
            S.dma(bgbias, bgb[l], w=["bgbias"])
            wcnt = 0
            fcount = 0
            groups = []
            for i in range(2):
                groups.append((1024 + i * 512, "q", i))
            for i in range(2):
                groups.append((2048 + i * 512, "k", i))
            for i in range(6):
                groups.append((5904 + i * 512, "bg", i))
            loaded = wload(groups[0][0], 512)
            for gidx, (c0, kind, gi) in enumerate(groups):
                wv, wn = loaded
                if gidx + 1 < len(groups):
                    loaded = wload(groups[gidx + 1][0], 512)
                for sub in range(4):
                    fb = gi * 4 + sub
                    k = fcount % 2
                    fcount += 1
                    for bi, (tb0, ntok) in enumerate(tokblocks()):
                        pp, ppn = ftype_block(wv, wn, tb0, ntok, col0=sub * 128)
                        dst = stg[k][:, tb0:tb0 + ntok]
                        eng_alt = (bi % 2 == 0)
                        if kind == "q":
                            S.copy("act" if eng_alt else "dve", dst, pp, r=[ppn], w=[f"stg{k}"])
                        elif kind == "k":
                            if eng_alt:
                                S.act(dst, pp, AF.Identity, r=[ppn], w=[f"stg{k}"], scale=0.0625)
                            else:
                                S.ts("dve", dst, pp, 0.0625, None, ALU.mult, None, r=[ppn], w=[f"stg{k}"])
                        else:
                            S.act(dst, pp, AF.Sigmoid, r=[ppn, "bgbias"], w=[f"stg{k}"], bias=bgbias[:, fb:fb + 1])
                    dd = {"q": mQT, "k": mKT, "bg": bgT}[kind]
                    S.store(dd[fb * 128:(fb + 1) * 128, :], stg[k], r=[f"stg{k}"], w=[f"{kind}T{fb}"])
            S.barrier()
            if stop_phase <= 3:
                break

            A.reset()
            stage = A.f32(8 * 512)
            wb = [A.bf16(8 * 512), A.bf16(8 * 512)]
            ost = [A.bf16(512), A.bf16(512)]
            wcnt = 0
            tcount = 0

            def ttype_block(wv, wn, tt, ncols):
                nonlocal pscnt
                pa, pn = P(pscnt % 4)
                pscnt += 1
                for kc in range(8):
                    S.mm(pa[:, 0:ncols], hT[:, kc, tt * 128:(tt + 1) * 128], wv[:, kc, 0:ncols], kc == 0, kc == 7,
                         r=[wn, f"hT{tt}"], w=[pn])
                return pa[:, 0:ncols], pn

            tg = []
            for i in range(2):
                tg.append((2048 + i * 512, "mk", i))
            for i in range(2):
                tg.append((3072 + i * 512, "mv", i))
            for i in range(2):
                tg.append((4096 + i * 512, "mo", i))
            tg.append((5776, "av", 0))
            ncol_of = lambda kind_: 128 if kind_ == "av" else 512
            loaded = wload(tg[0][0], ncol_of(tg[0][1]))
            wq_pre = None
            for gidx, (c0, kind, gi) in enumerate(tg):
                ncol = ncol_of(kind)
                wv, wn = loaded
                if gidx + 1 < len(tg):
                    loaded = wload(tg[gidx + 1][0], ncol_of(tg[gidx + 1][1]))
                else:
                    wq_pre = wload(5136, 512)
                for tt in range(NT):
                    pp, ppn = ttype_block(wv, wn, tt, ncol)
                    k = tcount % 2
                    tcount += 1
                    if kind == "mk":
                        if tt % 2 == 0:
                            S.act(ost[k], pp, AF.Identity, r=[ppn], w=[f"ost{k}"], scale=0.0625)
                        else:
                            S.ts("dve", ost[k], pp, 0.0625, None, ALU.mult, None, r=[ppn], w=[f"ost{k}"])
                        dd = mK
                    elif kind == "av":
                        S.copy("act" if tt % 2 == 0 else "dve", ost[k][:, 0:128], pp, r=[ppn], w=[f"ost{k}"])
                        S.store(aV[tt * 128:(tt + 1) * 128, :], ost[k][:, 0:128], r=[f"ost{k}"], w=[f"aV{tt}"])
                        continue
                    elif kind == "mv":
                        S.copy("act" if tt % 2 == 0 else "dve", ost[k], pp, r=[ppn], w=[f"ost{k}"])
                        dd = mV
                    else:
                        S.act(ost[k], pp, AF.Sigmoid, r=[ppn], w=[f"ost{k}"])
                        dd = mO
                    S.store(dd[tt * 128:(tt + 1) * 128, gi * 512:(gi + 1) * 512], ost[k], r=[f"ost{k}"], w=[f"{kind}{tt}_{gi}"])

            CUT = _CACHE.get("cut", 99)
            qnb = A.f32(64); knb = A.f32(64); gbb = A.f32(16)
            S.dma(qnb, qnw[l, :].partition_broadcast(128), w=["qnb"])
            S.dma(knb, knw[l, :].partition_broadcast(128), w=["knb"])
            S.dma(gbb, gate_b[l, :].partition_broadcast(128), w=["gbb"])
            cs = [A.f32(64), A.f32(64)]
            sq = A.f32(512); ssq = [A.f32(8), A.f32(8)]
            qn = [A.f32(512), A.f32(512)]
            ta = A.f32(256); tb_ = A.f32(256); tc = A.f32(256); td = A.f32(256)
            qr = [A.bf16(512), A.bf16(512)]
            vst = [A.bf16(128), A.bf16(128)]
            qTs = A.bf16(4 * NTOK); kTs = A.bf16(NTOK)
            qTsv = qTs.rearrange("p (j t) -> p j t", j=4)

            def qknorm_rope(pp, ppn, nh, wbc, wbn, tt, k, tag):
                n = nh * 64
                v3 = lambda ap: ap[:, 0:n].rearrange("p (h d) -> p h d", h=nh)
                S.act(sq[:, 0:n], pp, AF.Square, r=[ppn], w=["sq"])
                S.reduce_sum(ssq[k][:, 0:nh], v3(sq), r=["sq"], w=[f"ssq{k}"])
                tiny_rstd(ssq[k][:, 0:nh], nh, 64, f"ssq{k}", [f"ssq{k}"])
                S.tt("dve", v3(qn[k]), v3(pp), ssq[k][:, 0:nh].unsqueeze(2).broadcast_to([128, nh, 64]), ALU.mult,
                     r=[ppn, f"ssq{k}"], w=[f"qn{k}"])
                S.tt("pool", v3(qn[k]), v3(qn[k]), wbc.unsqueeze(1).broadcast_to([128, nh, 64]), ALU.mult,
                     r=[f"qn{k}", wbn], w=[f"qn{k}"])
                x1 = v3(qn[k])[:, :, 0:32]; x2 = v3(qn[k])[:, :, 32:64]
                cc = cs[tt % 2][:, 0:32].unsqueeze(1).broadcast_to([128, nh, 32])
                sn = cs[tt % 2][:, 32:64].unsqueeze(1).broadcast_to([128, nh, 32])
                h3 = lambda ap: ap[:, 0:nh * 32].rearrange("p (h d) -> p h d", h=nh)
                csn = f"cs{tt % 2}"
                S.tt("dve", h3(ta), x1, cc, ALU.mult, r=[f"qn{k}", csn], w=["ta"])
                S.tt("pool", h3(tb_), x2, sn, ALU.mult, r=[f"qn{k}", csn], w=["tb"])
                S.tt("dve", h3(tc), x1, sn, ALU.mult, r=[f"qn{k}", csn], w=["tc"])
                S.tt("pool", h3(td), x2, cc, ALU.mult, r=[f"qn{k}", csn], w=["td"])
                o3 = v3(qr[k])
                S.tt("dve", o3[:, :, 0:32], h3(ta), h3(tb_), ALU.subtract, r=["ta", "tb"], w=[f"qr{k}a"])
                S.tt("pool", o3[:, :, 32:64], h3(tc), h3(td), ALU.add, r=["tc", "td"], w=[f"qr{k}b"])

            wq, wqn = wq_pre
            wr, wrn = wload(5120, 16, extra=[(5648, 256)])
            for tt in range(NT if CUT > 0 else 0):
                k = tt % 2
                S.dma(cs[k], rope_d[tt * 128:(tt + 1) * 128, :], w=[f"cs{k}"])
                pp, ppn = ttype_block(wq, wqn, tt, 512)
                if CUT > 1:
                    qknorm_rope(pp, ppn, 8, qnb, "qnb", tt, k, "q")
                for half in range(2 if CUT > 2 else 0):
                    pa, pn = P(4 + half)
                    for q in range(2):
                        j = half * 2 + q
                        S.mm(pa[:, q * 128:(q + 1) * 128], qr[k][:, j * 128:(j + 1) * 128], identb[:], True, True,
                             r=[f"qr{k}a", f"qr{k}b", "identb"], w=[pn])
                    S.copy("act", qTsv[:, half * 2:half * 2 + 2, tt * 128:(tt + 1) * 128],
                           pa[:, 0:256].rearrange("p (q t) -> p q t", q=2), r=[pn], w=["qTs"])
                if CUT <= 3:
                    continue
                pgt, pgn = P(7)
                for kc in range(8):
                    S.mm(pgt[:, 0:16], hT[:, kc, tt * 128:(tt + 1) * 128], wr[:, kc, 0:16], kc == 0, kc == 7, r=[wrn, f"hT{tt}"], w=[pgn])
                S.tt("dve", Gt[:, tt, :], pgt[:, 0:16], gbb, ALU.add, r=[pgn, "gbb"], w=["Gt"])
                pa, pn = P(pscnt % 4)
                pscnt += 1
                for kc in range(8):
                    S.mm(pa[:, 0:128], hT[:, kc, tt * 128:(tt + 1) * 128], wr[:, kc, 16:144], kc == 0, kc == 7, r=[wrn, f"hT{tt}"], w=[pn])
                if CUT <= 4:
                    continue
                qknorm_rope(pa[:, 0:128], pn, 2, knb, "knb", tt, k, "k")
                if CUT <= 4.2:
                    continue
                pk, pkn = P(6)
                S.mm(pk[:, 0:128], qr[k][:, 0:128], identb[:], True, True, r=[f"qr{k}a", f"qr{k}b", "identb"], w=[pkn])
                if CUT <= 4.4:
                    continue
                S.copy("act", kTs[:, tt * 128:(tt + 1) * 128], pk[:, 0:128], r=[pkn], w=["kTs"])
            for j in range(4):
                S.store(aQT[j * 128:(j + 1) * 128, :], qTsv[:, j, :], r=["qTs"], w=[f"aQT{j}"])
            if CUT > 4.6:
                S.store(aKT, kTs, r=["kTs"], w=["aKT"])
            if debug and l == 0:
                S.dma(dbg_gt, Gt_t[:], r=["Gt"], w=["dbg_gt"])
            S.barrier()
            if stop_phase <= 4:
                break

            A.reset()
            lnwt = A.f32(4); lnbt = A.f32(4)
            S.dma(lnwt, lnw[l], w=["lnwt"]); S.dma(lnbt, lnb[l], w=["lnbt"])
            cv = [A.f32(4 * 512), A.f32(4 * 512)]
            sqv = A.f32(4 * 512)
            mean = A.f32(512); var = A.f32(512); msq = A.f32(512)
            tmpn = [A.f32(512), A.f32(512)]
            hst = [A.bf16(512), A.bf16(512)]
            for bi, (tb0, ntok) in enumerate(tokblocks()):
                k = bi % 2
                cvv = cv[k].rearrange("p (j t) -> p j t", j=4)
                S.dma(cvv[:, :, 0:ntok], convT[:, tb0:tb0 + ntok].rearrange("(j p) t -> p j t", p=128), w=[f"cv{k}"])
                pa, pan = P(0 + 2 * k); pb, pbn = P(1 + 2 * k)
                for j in range(4):
                    S.mm(pa[:, 0:ntok], ones[:], cvv[:, j, 0:ntok], j == 0, j == 3, r=["ones", f"cv{k}"], w=[pan])
                sqq = sqv.rearrange("p (j t) -> p j t", j=4)
                S.act(sqq[:, :, 0:ntok], cvv[:, :, 0:ntok], AF.Square, r=[f"cv{k}"], w=["sqv"])
                for j in range(4):
                    S.mm(pb[:, 0:ntok], ones[:], sqq[:, j, 0:ntok], j == 0, j == 3, r=["ones", "sqv"], w=[pbn])
                S.ts("dve", mean[:, 0:ntok], pa[:, 0:ntok], 1.0 / 512, None, ALU.mult, None, r=[pan], w=["mean"])
                S.tt("dve", msq[:, 0:ntok], mean[:, 0:ntok], mean[:, 0:ntok], ALU.mult, r=["mean"], w=["msq"])
                S.stt("dve", var[:, 0:ntok], pb[:, 0:ntok], 1.0 / 512, msq[:, 0:ntok], ALU.mult, ALU.subtract, r=[pbn, "msq"], w=["var"])
                S.ts("dve", var[:, 0:ntok], var[:, 0:ntok], EPS, None, ALU.add, None, r=["var"], w=["var"])
                S.act(var[:, 0:ntok], var[:, 0:ntok], AF.Sqrt, r=["var"], w=["var"])
                S.recip(var[:, 0:ntok], var[:, 0:ntok], r=["var"], w=["var"])
                for j in range(4):
                    kk = j % 2
                    S.tt("dve", tmpn[kk][:, 0:ntok], cvv[:, j, 0:ntok], mean[:, 0:ntok], ALU.subtract, r=[f"cv{k}", "mean"], w=[f"tmpn{kk}"])
                    S.tt("pool", tmpn[kk][:, 0:ntok], tmpn[kk][:, 0:ntok], var[:, 0:ntok], ALU.mult, r=[f"tmpn{kk}", "var"], w=[f"tmpn{kk}"])
                    S.act(hst[kk][:, 0:ntok], tmpn[kk][:, 0:ntok], AF.Silu, r=[f"tmpn{kk}", "lnwt", "lnbt"], w=[f"hst{kk}"],
                          scale=lnwt[:, j:j + 1], bias=lnbt[:, j:j + 1])
                    S.store(hcT[j * 128:(j + 1) * 128, tb0:tb0 + ntok], hst[kk][:, 0:ntok], r=[f"hst{kk}"], w=[f"hcT{j}_{bi}"])
            S.barrier()
            if stop_phase <= 5:
                break

            A.reset()
            Gli = [Gt[:, :, 0:4], Gt[:, :, 8:12]]
            Gpf = [Gt[:, :, 4:8], Gt[:, :, 12:16]]
            WI = []; DEC = []; RR = []; WS = []
            NC4 = NT * 4
            for d in range(2):
                nlf = A.f32(NC4); cums = A.f32(NC4); d1 = A.f32(NC4); d2 = A.f32(NC4)
                wi = A.f32(NC4); dec = A.f32(NC4); rr = A.f32(NC4); ws = A.f32(NC4)
                v34 = lambda ap: ap.rearrange("p (t h) -> p t h", h=4)
                S.act(v34(nlf), Gpf[d], AF.Exp, r=["Gt"], w=[f"nlf{d}"], scale=-1.0)
                S.act(nlf, nlf, AF.Ln, r=[f"nlf{d}"], w=[f"nlf{d}"], bias=1.0)
                pa, pan = P(7); pb, pbn = P(6)
                S.mm(pa[:, 0:NC4], (triU if d == 0 else triL)[:], nlf, True, True, r=["triU", "triL", f"nlf{d}"], w=[pan])
                S.mm(pb[:, 0:NC4], ones[:], nlf, True, True, r=["ones", f"nlf{d}"], w=[pbn])
                S.copy("act", cums, pa[:, 0:NC4], r=[pan], w=[f"cums{d}"])
                S.act(wi, pa[:, 0:NC4], AF.Exp, r=[pan], w=[f"WI{d}"], scale=-1.0)
                S.act(dec, pb[:, 0:NC4], AF.Exp, r=[pbn], w=[f"DEC{d}"], scale=-1.0)
                S.tt("dve", d1, pb[:, 0:NC4], cums, ALU.subtract, r=[pbn, f"cums{d}"], w=[f"d1{d}"])
                S.act(rr, d1, AF.Exp, r=[f"d1{d}"], w=[f"R{d}"])
                S.tt("dve", v34(d2), Gli[d], v34(d1), ALU.subtract, r=["Gt", f"d1{d}"], w=[f"d2{d}"])
                S.act(ws, d2, AF.Exp, r=[f"d2{d}"], w=[f"WS{d}"])
                WI.append(wi); DEC.append(dec); RR.append(rr); WS.append(ws)
                if debug and l == 0:
                    for qi, (tl, tn) in enumerate(((wi, f"WI{d}"), (dec, f"DEC{d}"), (rr, f"R{d}"), (ws, f"WS{d}"))):
                        o_ = (d * 4 + qi) * NC4
                        S.dma(dbg_gates[:, o_:o_ + NC4], tl, r=[tn], w=[f"dbg_gates{d}{qi}"])
            masks = [triU, triL]
            Cs = [A.f32(2 * 257) for _ in range(4)]
            Cbs = [A.bf16(2 * 258) for _ in range(4)]
            Cv = [c.rearrange("p (c e) -> p c e", c=2) for c in Cs]
            Cbv = [c.rearrange("p (c e) -> p c e", c=2) for c in Cbs]
            nwbm = A.f32(D)
            S.dma(nwbm, mnorm_w[l, :].partition_broadcast(128), w=["nwbm"])
            QTb = [A.bf16(8 * 256) for _ in range(2)]
            KTb = [A.bf16(8 * 256) for _ in range(2)]
            Ktb = [A.bf16(2 * 1024) for _ in range(2)]
            V1b = [A.bf16(2 * 4 * 257) for _ in range(2)]
            hFb = [A.f32(2 * 1024) for _ in range(2)]
            sOb = [A.bf16(2 * 1024) for _ in range(2)]
            hmTsb = [A.bf16(8 * 256) for _ in range(2)]
            QTv = [q.rearrange("p (g t) -> p g t", g=8) for q in QTb]
            KTv = [q.rearrange("p (g t) -> p g t", g=8) for q in KTb]
            Ktv = [q.rearrange("p (a f) -> p a f", a=2) for q in Ktb]
            V1v = [q.rearrange("p (a h e) -> p a h e", a=2, h=4) for q in V1b]
            hFv = [q.rearrange("p (a f) -> p a f", a=2) for q in hFb]
            sOv = [q.rearrange("p (a f) -> p a f", a=2) for q in sOb]
            hmTsv = [q.rearrange("p (g t) -> p g t", g=8) for q in hmTsb]
            PTb = [A.bf16(128) for _ in range(4)]
            tmpb = [A.f32(257) for _ in range(4)]
            numb = [A.f32(257) for _ in range(4)]
            denb = [A.f32(2) for _ in range(4)]
            Kwb = [A.bf16(256) for _ in range(4)]
            hsb = [A.f32(256) for _ in range(4)]
            hnb = [A.f32(256) for _ in range(4)]
            hgb = [A.bf16(256) for _ in range(4)]
            ssb = [A.f32(1) for _ in range(4)]
            junk4 = A.f32(256)
            for k in range(2):
                S.memset("pool", V1b[k], 1.0, w=[f"V1{k}"])
            mQTv = mQT.rearrange("(g p) t -> p g t", p=128)
            mKTv = mKT.rearrange("(g p) t -> p g t", p=128)
            hmTv = hmT.rearrange("(g p) t -> p g t", p=128)
            spans_f = [(0, [0, 1])] + [(2 + 2 * i, [0, 1]) for i in range(16)]
            spans_b = [(0, [1, 0])] + [(2 + 2 * i, [1, 0]) for i in reversed(range(16))]
            scnt = 0
            deferred = []
            for d in range(2):
                for h in range(4):
                    S.memset("pool", Cs[h], 0.0, w=[f"C{h}"])
                    S.memset("pool", Cbs[h], 0.0, w=[f"Cb{h}"])
                for (st0, order) in (spans_f if d == 0 else spans_b):
                    k = scnt % 2
                    scnt += 1
                    t0 = st0 * 128
                    S.dma(QTv[k], mQTv[:, :, t0:t0 + 256], w=[f"QT{k}"])
                    S.dma(KTv[k], mKTv[:, :, t0:t0 + 256], w=[f"KT{k}"])
                    S.dma(Ktv[k], mK[t0:t0 + 256, :].rearrange("(a p) f -> p a f", p=128), w=[f"Kt{k}"])
                    for a_ in range(2):
                        S.dma(V1v[k][:, a_, :, 0:256], mV[t0 + a_ * 128:t0 + (a_ + 1) * 128, :].rearrange("p (h e) -> p h e", h=4), w=[f"V1{k}"])
                    if d == 1:
                        S.dma(hFv[k], hF[t0:t0 + 256, :].rearrange("(a p) f -> p a f", p=128), w=[f"hF{k}"])
                        S.dma(sOv[k], mO[t0:t0 + 256, :].rearrange("(a p) f -> p a f", p=128), w=[f"sO{k}"])
                    for a in order:
                        tt = st0 + a
                        tsl = slice(a * 128, (a + 1) * 128)
                        for pair in range(2):
                            hp = (2 * pair, 2 * pair + 1)
                            col = {h: tt * 4 + h for h in hp}
                            hsl = {h: slice(h * 256, (h + 1) * 256) for h in hp}
                            pst, pstn = P(0) if pair == 0 else P(6)
                            pqc = {hp[0]: P(1), hp[1]: P(2)}
                            psv = {hp[0]: P(3), hp[1]: P(4)}
                            pdc = (P(5), P(7))
                            for h in hp:
                                S.act(Kwb[h], Ktv[k][:, a, hsl[h]], AF.Identity, r=[f"Kt{k}", f"WS{d}"], w=[f"Kw{h}"],
                                      scale=WS[d][:, col[h]:col[h] + 1])
                            for i_, h in enumerate(hp):
                                for dc in range(2):
                                    S.mm(pst[:, i_ * 128:(i_ + 1) * 128], KTv[k][:, h * 2 + dc, tsl], QTv[k][:, h * 2 + dc, tsl], dc == 0, dc == 1,
                                         r=[f"KT{k}", f"QT{k}"], w=[pstn])
                            for h in hp:
                                pq, pqn = pqc[h]
                                for dc in range(2):
                                    S.mm(pq[:, 0:257], QTv[k][:, h * 2 + dc, tsl], Cbv[h][:, dc, 0:257], dc == 0, dc == 1,
                                         r=[f"QT{k}", f"Cb{h}"], w=[pqn])
                            for i_, h in enumerate(hp):
                                S.stt("dve", PTb[h], pst[:, i_ * 128:(i_ + 1) * 128], WS[d][:, col[h]:col[h] + 1], masks[d][:], ALU.mult, ALU.mult,
                                      r=[pstn, f"WS{d}", "triU", "triL"], w=[f"PT{h}"])
                            for h in hp:
                                pq, pqn = pqc[h]
                                S.act(tmpb[h], pq[:, 0:257], AF.Identity, r=[pqn, f"WI{d}"], w=[f"tmp{h}"], scale=WI[d][:, col[h]:col[h] + 1])
                            for i_, h in enumerate(hp):
                                ps_, psn_ = psv[h]
                                S.mm(ps_[:, 0:257], PTb[h], V1v[k][:, a, h, :], True, True, r=[f"PT{h}", f"V1{k}"], w=[psn_])
                            for f_ in deferred:
                                f_()
                            deferred.clear()
                            for i_, h in enumerate(hp):
                                for dc in range(2):
                                    pd_, pdn_ = pdc[dc]
                                    S.mm(pd_[:, 0:257], Kwb[h][:, dc * 128:(dc + 1) * 128], V1v[k][:, a, h, :], True, True,
                                         r=[f"Kw{h}", f"V1{k}"], w=[pdn_])
                                    S.stt("dve", Cv[h][:, dc, :], Cv[h][:, dc, :], DEC[d][:, col[h]:col[h] + 1], pd_[:, 0:257], ALU.mult, ALU.add,
                                          r=[f"C{h}", f"DEC{d}", pdn_], w=[f"C{h}"])
                                S.copy("act", Cbv[h][:, :, 0:257], Cv[h], r=[f"C{h}"], w=[f"Cb{h}"])
                                ps_, psn_ = psv[h]
                                S.stt("dve", numb[h], ps_[:, 0:257], RR[d][:, col[h]:col[h] + 1], tmpb[h], ALU.mult, ALU.add,
                                      r=[psn_, f"R{d}", f"tmp{h}"], w=[f"num{h}"])
                                S.stt("dve", denb[h][:, 1:2], numb[h][:, 256:257], -1.0, numb[h][:, 256:257], ALU.mult, ALU.max, r=[f"num{h}"], w=[f"den{h}"])
                                S.ts("dve", denb[h][:, 1:2], denb[h][:, 1:2], 1.0, None, ALU.max, None, r=[f"den{h}"], w=[f"den{h}"])
                                S.recip(denb[h][:, 1:2], denb[h][:, 1:2], r=[f"den{h}"], w=[f"den{h}"])

                                def readout(h=h, k=k, a=a, tsl=tsl, d=d, hs_=hsl[h]):
                                    rc = denb[h][:, 1:2]
                                    if d == 0:
                                        S.act(hFv[k][:, a, hs_], numb[h][:, 0:256], AF.Identity, r=[f"num{h}", f"den{h}"], w=[f"hF{k}"], scale=rc)
                                        return
                                    S.stt("dve", hsb[h], numb[h][:, 0:256], rc, hFv[k][:, a, hs_], ALU.mult, ALU.add,
                                          r=[f"num{h}", f"den{h}", f"hF{k}"], w=[f"hs{h}"])
                                    S.act(junk4, hsb[h], AF.Square, r=[f"hs{h}"], w=["junk4", f"ss4{h}"], accum=ssb[h])
                                    tiny_rstd(ssb[h], 1, 256, f"ss4{h}", [f"ss4{h}"])
                                    S.stt("dve", hnb[h], hsb[h], ssb[h], nwbm[:, hs_], ALU.mult, ALU.mult,
                                          r=[f"hs{h}", f"ss4{h}", "nwbm"], w=[f"hn{h}"])
                                    S.tt("pool", hgb[h], hnb[h], sOv[k][:, a, hs_], ALU.mult, r=[f"hn{h}", f"sO{k}"], w=[f"hg{h}"])
                                    pt_, ptn_ = P(0)
                                    for dc in range(2):
                                        S.mm(pt_[:, 256 + dc * 128:256 + (dc + 1) * 128], hgb[h][:, dc * 128:(dc + 1) * 128], identb[:], True, True,
                                             r=[f"hg{h}", "identb"], w=[ptn_])
                                    S.copy("dve", hmTsv[k][:, h * 2:h * 2 + 2, tsl], pt_[:, 256:512].rearrange("p (c t) -> p c t", c=2),
                                           r=[ptn_], w=[f"hmTs{k}"])

                                deferred.append(readout)
                    for f_ in deferred:
                        f_()
                    deferred.clear()
                    if d == 0:
                        S.store(hF[t0:t0 + 256, :].rearrange("(a p) f -> p a f", p=128), hFv[k], r=[f"hF{k}"], w=[f"hFd{st0}"])
                    else:
                        S.store(hmTv[:, :, t0:t0 + 256], hmTsv[k], r=[f"hmTs{k}"], w=[f"hmT{st0}"])
                if d == 0:
                    S.barrier()
            S.barrier()
            if stop_phase <= 6:
                break

            A.reset()
            KTa = A.bf16(2 * NTOK)
            KTav = KTa.rearrange("p (v t) -> p v t", v=2)
            V1a = A.bf16(NT * 2 * 128)
            V1av = V1a.rearrange("p (a v e) -> p a v e", a=NT, v=2)
            Rfull = A.f32(512)
            Rf2 = [A.f32(512), A.f32(512)]
            rb2 = [A.f32(512), A.f32(512)]
            QTq = [A.bf16(512), A.bf16(512)]
            Ptb = [A.bf16(512) for _ in range(8)]
            rbt = A.f32(512)
            aost = [A.bf16(512), A.bf16(512)]
            S.dma(KTav[0:64, :, :], aKT.rearrange("(v d) t -> d v t", d=64), w=["KTa"])
            S.dma(KTav[64:128, :, :], aKT.rearrange("(v d) t -> d v t", d=64), w=["KTa"])
            S.memset("pool", V1a, 1.0, w=["V1a"])
            for v_ in range(2):
                S.dma(V1av[:, :, v_, 0:64], aV[:, v_ * 64:(v_ + 1) * 64].rearrange("(a p) e -> p a e", p=128), w=["V1a"])
            S.memset("pool", Rfull, 0.0, w=["Rfull"])
            blocks = [(0, 256, [0, 1])] + [(256 + 512 * i, 512, list(range(NT))) for i in range(8)]
            if last:
                blocks = blocks[1:]
            bcnt = 0
            for hpair in range(4):
                kv = hpair // 2
                for (q0, nq, ktiles) in blocks:
                    k = bcnt % 2
                    bcnt += 1
                    S.dma(QTq[k][:, 0:nq], aQT[hpair * 128:(hpair + 1) * 128, q0:q0 + nq], w=[f"QTq{k}"])
                    pos = (P(6), P(7))
                    nkt = len(ktiles)
                    def emit_qk(i):
                        kt = ktiles[i]
                        for hd in range(2):
                            ps_, psn = P((i % 3) * 2 + hd)
                            psl = slice(hd * 64, (hd + 1) * 64)
                            S.mm(ps_[:, 0:nq], KTav[psl, kv, kt * 128:(kt + 1) * 128], QTq[k][psl, 0:nq], True, True,
                                 r=["KTa", f"QTq{k}"], w=[psn])
                        for hd in range(2):
                            ps_, psn = P((i % 3) * 2 + hd)
                            sl_ = (i % 4) * 2 + hd
                            S.act(Ptb[sl_][:, 0:nq], ps_[:, 0:nq], AF.Exp, r=[psn], w=[f"Pt{sl_}"], scale=0.125)

                    for i in range(min(3, nkt)):
                        emit_qk(i)
                    for i in range(nkt):
                        if i + 3 < nkt:
                            emit_qk(i + 3)
                        for hd in range(2):
                            sl_ = (i % 4) * 2 + hd
                            po, pon = pos[hd]
                            S.mm(po[:, 0:nq], V1av[:, ktiles[i], kv, :], Ptb[sl_][:, 0:nq], i == 0, i == nkt - 1,
                                 r=["V1a", f"Pt{sl_}"], w=[pon])
                    for hd in range(2):
                        h = hpair * 2 + hd
                        po, pon = pos[hd]
                        rf = Rf2[hd]; rb_ = rb2[hd]
                        S.recip(rf[64:128, 0:nq], po[64:128, 0:nq], r=[pon], w=[f"Rf{hd}"])
                        S.dma(rb_[0:64, 0:nq], rf[64:128, 0:nq], r=[f"Rf{hd}"], w=[f"rb{hd}"])
                        S.tt("dve", aost[hd][0:64, 0:nq], po[0:64, 0:nq], rb_[0:64, 0:nq], ALU.mult, r=[pon, f"rb{hd}"], w=[f"aost{hd}"])
                        S.store(aoT[h * 64:(h + 1) * 64, q0:q0 + nq], aost[hd][0:64, 0:nq], r=[f"aost{hd}"], w=[f"aoT{h}_{q0}"])
            S.barrier()
            if stop_phase <= 7:
                break

            A.reset()
            stage = A.f32(4096)
            stv = stage.rearrange("p (k f) -> p k f", k=4)
            wco = A.bf16(4 * D); wmo = A.bf16(8 * D); wao = A.bf16(4 * D)
            wcov = wco.rearrange("p (k f) -> p k f", k=4)
            wmov = wmo.rearrange("p (k f) -> p k f", k=8)
            waov = wao.rearrange("p (k f) -> p k f", k=4)
            pieces = [(w_conv_out, 0, wcov, 0, "wco"), (w_mlstm_out, 0, wmov, 0, "wmo"), (w_mlstm_out, 4, wmov, 4, "wmo"),
                      (w_attn_out, 0, waov, 0, "wao")]
            hcnt = 0
            for pi, (wsrc, k0, dstv, dk0, dn) in enumerate(pieces):
                for hh in range(2):
                    sh = hcnt % 2
                    hcnt += 1
                    S.dma(stv[:, sh * 2:sh * 2 + 2, :], wsrc[l].rearrange("(k p) f -> p k f", p=128)[:, k0 + hh * 2:k0 + hh * 2 + 2, :], w=[f"wstage{sh}"])
                    S.copy("dve" if sh == 0 else "act", dstv[:, dk0 + hh * 2:dk0 + hh * 2 + 2, :], stv[:, sh * 2:sh * 2 + 2, :], r=[f"wstage{sh}"],
                           w=[dn + str(dk0)] if hh == 1 else [dn + str(dk0) + "h"])
            wnames = ["wco0", "wmo0", "wmo4", "wao0", "wco0h", "wmo0h", "wmo4h", "wao0h"]
            opnd = [A.bf16(16 * 512), A.bf16(16 * 512)]
            opv = [o.rearrange("p (k t) -> p k t", k=16) for o in opnd]
            gtb = [A.bf16(3 * 512), A.bf16(3 * 512)]
            gtv = [g.rearrange("p (b t) -> p b t", b=3) for g in gtb]
            m1s = [A.f32(512), A.f32(512)]; m2s = [A.f32(512), A.f32(512)]; m3s = [A.f32(512), A.f32(512)]
            mst = [A.bf16(512), A.bf16(512)]
            bgTv = bgT.rearrange("(b f p) t -> p b f t", b=3, p=128)
            fcnt = 0
            for bi, (tb0, ntok) in enumerate(tokblocks()):
                if last and bi == 0:
                    continue
                k = bi % 2
                S.dma(opv[k][:, 0:4, 0:ntok], hcT[:, tb0:tb0 + ntok].rearrange("(k p) t -> p k t", p=128), w=[f"op{k}"])
                S.dma(opv[k][:, 4:12, 0:ntok], hmT[:, tb0:tb0 + ntok].rearrange("(k p) t -> p k t", p=128), w=[f"op{k}"])
                S.dma(opv[k][:, 12:16, 0:ntok], aoT[:, tb0:tb0 + ntok].rearrange("(k p) t -> p k t", p=128), w=[f"op{k}"])
                for fb in range(8):
                    k2 = fcnt % 2
                    fcnt += 1
                    fsl = slice(fb * 128, (fb + 1) * 128)
                    S.dma(gtv[k2][:, :, 0:ntok], bgTv[:, :, fb, tb0:tb0 + ntok], w=[f"gt{k2}"])
                    pc, pcn = P(3 * k2); pm, pmn = P(3 * k2 + 1); pa, pan = P(3 * k2 + 2)
                    for kc in range(4):
                        S.mm(pc[:, 0:ntok], wcov[:, kc, fsl], opv[k][:, kc, 0:ntok], kc == 0, kc == 3, r=wnames + [f"op{k}"], w=[pcn])
                    for kc in range(8):
                        S.mm(pm[:, 0:ntok], wmov[:, kc, fsl], opv[k][:, 4 + kc, 0:ntok], kc == 0, kc == 7, r=wnames + [f"op{k}"], w=[pmn])
                    for kc in range(4):
                        S.mm(pa[:, 0:ntok], waov[:, kc, fsl], opv[k][:, 12 + kc, 0:ntok], kc == 0, kc == 3, r=wnames + [f"op{k}"], w=[pan])
                    m1 = m1s[k2]; m2 = m2s[k2]; m3 = m3s[k2]
                    S.tt("dve", m1[:, 0:ntok], pc[:, 0:ntok], gtv[k2][:, 0, 0:ntok], ALU.mult, r=[pcn, f"gt{k2}"], w=[f"m1{k2}"])
                    S.tt("dve", m2[:, 0:ntok], pm[:, 0:ntok], gtv[k2][:, 1, 0:ntok], ALU.mult, r=[pmn, f"gt{k2}"], w=[f"m2{k2}"])
                    S.tt("pool", m1[:, 0:ntok], m1[:, 0:ntok], m2[:, 0:ntok], ALU.add, r=[f"m1{k2}", f"m2{k2}"], w=[f"m1{k2}"])
                    S.tt("dve", m3[:, 0:ntok], pa[:, 0:ntok], gtv[k2][:, 2, 0:ntok], ALU.mult, r=[pan, f"gt{k2}"], w=[f"m3{k2}"])
                    S.tt("pool", mst[k2][:, 0:ntok], m1[:, 0:ntok], m3[:, 0:ntok], ALU.add, r=[f"m1{k2}", f"m3{k2}"], w=[f"mst{k2}"])
                    S.store(mergedT[fsl, tb0:tb0 + ntok], mst[k2][:, 0:ntok], r=[f"mst{k2}"], w=[f"mgT{fb}_{bi}"])
            S.barrier()
            if stop_phase <= 8:
                break

            A.reset()
            stage = A.f32(4096)
            stv = stage.rearrange("p (k f) -> p k f", k=4)
            wo = A.bf16(8 * D)
            wov = wo.rearrange("p (k f) -> p k f", k=8)
            for pi in range(4):
                sh = pi % 2
                S.dma(stv[:, sh * 2:sh * 2 + 2, :], w_out[l].rearrange("(k p) f -> p k f", p=128)[:, pi * 2:pi * 2 + 2, :], w=[f"wstage{sh}"])
                S.copy("dve" if sh == 0 else "act", wov[:, pi * 2:pi * 2 + 2, :], stv[:, sh * 2:sh * 2 + 2, :], r=[f"wstage{sh}"], w=[f"wo{pi}"])
            mgb = [A.bf16(8 * 512), A.bf16(8 * 512)]
            mgv = [m.rearrange("p (k t) -> p k t", k=8) for m in mgb]
            modl3 = A.f32(6 * D).rearrange("p (r c f) -> p r c f", r=2, c=3)
            S.dma(modl3, moddv[:, :, 2:5, :], w=["modl"])
            xt = [A.f32(D), A.f32(D)]
            t1g = [A.f32(512), A.f32(512)]
            x1 = [A.f32(D), A.f32(D)]
            nb = norm_bufs()
            tcnt = 0
            for bi, (tb0, ntok) in enumerate(tokblocks()):
                if last and bi == 0:
                    continue
                k = bi % 2
                S.dma(mgv[k][:, :, 0:ntok], mergedT[:, tb0:tb0 + ntok].rearrange("(k p) t -> p k t", p=128), w=[f"mg{k}"])
                for a in range(ntok // 128):
                    tt = tb0 // 128 + a
                    r = 1 if tt < 2 else 0
                    kx = tcnt % 2
                    tcnt += 1
                    S.dma(xt[kx], xsrc[tt * 128:(tt + 1) * 128, :], w=[f"xt{kx}"])
                    for half in range(2):
                        po, pon = P(4 + half + 2 * kx)
                        hs_ = slice(half * 512, (half + 1) * 512)
                        for kc in range(8):
                            S.mm(po, mgv[k][:, kc, a * 128:(a + 1) * 128], wov[:, kc, hs_], kc == 0, kc == 7,
                                 r=[f"mg{k}", "wo0", "wo1", "wo2", "wo3"], w=[pon])
                        S.tt("dve", t1g[half], po, modl3[:, r, 0, hs_], ALU.mult, r=[pon, "modl"], w=[f"t1g{half}"])
                        S.tt("pool", x1[kx][:, hs_], t1g[half], xt[kx][:, hs_], ALU.add, r=[f"t1g{half}", f"xt{kx}"], w=[f"x1_{kx}"])
                    S.store(xres1[tt * 128:(tt + 1) * 128, :], x1[kx], r=[f"x1_{kx}"], w=[f"xres1_{tt}"])
                    norm_tile(x1[kx], f"x1_{kx}", tt, modl3[:, :, 1:3, :], nb, tcnt)
            S.barrier()
            if stop_phase <= 9:
                break

            A.reset()
            PT3 = NTOK + 3
            n3 = PT3 - 2
            stage = A.f32(8 * 256)
            stv8 = stage.rearrange("p (k c) -> p k c", k=8)
            wbu = [A.bf16(8 * 256), A.bf16(8 * 256)]
            wbuv = [w_.rearrange("p (k c) -> p k c", k=8) for w_ in wbu]
            uaB = [A.f32(PT3), A.f32(PT3)]; ugB = [A.f32(PT3), A.f32(PT3)]
            ca = A.f32(n3); cg = A.f32(n3)
            prod = A.bf16(n3)
            fwt = A.f32(132); fbt = A.f32(44)
            S.dma(fwt, fcw[l], w=["fwt"]); S.dma(fbt, fcb[l], w=["fbt"])
            uan = [[f"ua{k}_{bi}" for bi in range(9)] for k in range(2)]
            ugn = [[f"ug{k}_{bi}" for bi in range(9)] for k in range(2)]
            for k in range(2):
                S.memset("pool", uaB[k], 0.0, w=uan[k])
                S.memset("pool", ugB[k], 0.0, w=ugn[k])
            def ffn_wload(j):
                k = j % 2
                S.dma(stv8[:, :, 0:128], w_up[l, :, j * 128:(j + 1) * 128].rearrange("(k p) c -> p k c", p=128), w=["wstage"])
                S.dma(stv8[:, :, 128:256], w_up[l, :, DFF + j * 128:DFF + (j + 1) * 128].rearrange("(k p) c -> p k c", p=128), w=["wstage"])
                S.copy("act", wbuv[k], stv8, r=["wstage"], w=[f"wbu{k}"])

            ffn_wload(0)
            for j in range(22):
                k = j % 2
                ua = uaB[k]; ug = ugB[k]
                for bi, (tb0, ntok) in enumerate(tokblocks()):
                    if last and bi == 0:
                        continue
                    pa, pan = P((bi % 2) * 2); pg, pgn = P((bi % 2) * 2 + 1)
                    hr = hT_all[tb0 // 128:(tb0 + ntok) // 128]
                    for kc in range(8):
                        S.mm(pa[:, 0:ntok], wbuv[k][:, kc, 0:128], hT[:, kc, tb0:tb0 + ntok], kc == 0, kc == 7, r=[f"wbu{k}"] + hr, w=[pan])
                    for kc in range(8):
                        S.mm(pg[:, 0:ntok], wbuv[k][:, kc, 128:256], hT[:, kc, tb0:tb0 + ntok], kc == 0, kc == 7, r=[f"wbu{k}"] + hr, w=[pgn])
                    c0 = padpos(tb0, 1)
                    S.copy("act", ua[:, c0:c0 + ntok], pa[:, 0:ntok], r=[pan], w=[uan[k][bi]])
                    S.copy("act", ug[:, c0:c0 + ntok], pg[:, 0:ntok], r=[pgn], w=[ugn[k][bi]])
                if j + 1 < 22:
                    ffn_wload(j + 1)
                wa_ = lambda tap: fwt[:, j * 3 + tap:j * 3 + tap + 1]
                wg_ = lambda tap: fwt[:, (22 + j) * 3 + tap:(22 + j) * 3 + tap + 1]
                S.ts("dve", cg, ug[:, 1:1 + n3], wg_(1), fbt[:, 22 + j:23 + j], ALU.mult, ALU.add, r=ugn[k] + ["fwt", "fbt"], w=["cg"])
                S.stt("dve", cg, ug[:, 0:n3], wg_(0), cg, ALU.mult, ALU.add, r=ugn[k] + ["fwt", "cg"], w=["cg"])
                S.stt("dve", cg, ug[:, 2:2 + n3], wg_(2), cg, ALU.mult, ALU.add, r=ugn[k] + ["fwt", "cg"], w=["cg"])
                sgl = ug[:, 1:1 + n3]
                S.act(sgl, cg, AF.Silu, r=["cg"] + ugn[k], w=ugn[k])
                S.ts("dve", ca, ua[:, 1:1 + n3], wa_(1), fbt[:, j:j + 1], ALU.mult, ALU.add, r=uan[k] + ["fwt", "fbt"], w=["ca"])
                S.stt("dve", ca, ua[:, 0:n3], wa_(0), ca, ALU.mult, ALU.add, r=uan[k] + ["fwt", "ca"], w=["ca"])
                S.stt("dve", ca, ua[:, 2:2 + n3], wa_(2), ca, ALU.mult, ALU.add, r=uan[k] + ["fwt", "ca"], w=["ca"])
                S.flush_stores()
                S.tt("dve", prod, ca, sgl, ALU.mult, r=["ca"] + ugn[k], w=["prod"])
                S.memset("pool", ug[:, NCTX + 1:NCTX + 2], 0.0, w=ugn[k])
                S.store(prodT[j * 128:(j + 1) * 128, 0:NCTX], prod[:, 0:NCTX], r=["prod"], w=[f"prodT{j}"])
                S.store(prodT[j * 128:(j + 1) * 128, NCTX:NTOK], prod[:, NCTX + 1:NTOK + 1], r=["prod"], w=[f"prodT{j}"])
            S.barrier()
            if stop_phase <= 10:
                break

            A.reset()
            stage = A.f32(4096)
            stv = stage.rearrange("p (k f) -> p k f", k=4)
            wd = A.bf16(22 * D)
            wdv = wd.rearrange("p (k f) -> p k f", k=22)
            wdn = []
            wsrc = w_down[l].rearrange("(k p) f -> p k f", p=128)
            for pi, k0 in enumerate(range(0, 22, 2)):
                sh = pi % 2
                S.dma(stv[:, sh * 2:sh * 2 + 2, :], wsrc[:, k0:k0 + 2, :], w=[f"wstage{sh}"])
                S.copy("dve" if sh == 0 else "act", wdv[:, k0:k0 + 2, :], stv[:, sh * 2:sh * 2 + 2, :], r=[f"wstage{sh}"], w=[f"wd{pi}"])
                wdn.append(f"wd{pi}")
            prb = [A.bf16(22 * 512), A.bf16(22 * 512)]
            prv = [p_.rearrange("p (k t) -> p k t", k=22) for p_ in prb]
            g2l = A.f32(2 * D).rearrange("p (r f) -> p r f", r=2)
            S.dma(g2l, moddv[:, :, 5, :], w=["g2l"])
            xt = [A.f32(D), A.f32(D)]
            t1g = [A.f32(512), A.f32(512)]
            x2 = [A.f32(D), A.f32(D)]
            xdst = out_d if last else xres2
            tcnt = 0
            for bi, (tb0, ntok) in enumerate(tokblocks()):
                if last and bi == 0:
                    continue
                k = bi % 2
                prsrc = prodT[:, tb0:tb0 + ntok].rearrange("(k p) t -> p k t", p=128)
                for k0 in range(0, 22, 6):
                    nk = min(6, 22 - k0)
                    S.dma(prv[k][:, k0:k0 + nk, 0:ntok], prsrc[:, k0:k0 + nk, :], w=[f"pr{k}"])
                for a in range(ntok // 128):
                    tt = tb0 // 128 + a
                    r = 1 if tt < 2 else 0
                    kx = tcnt % 2
                    tcnt += 1
                    S.dma(xt[kx], xres1[tt * 128:(tt + 1) * 128, :], w=[f"xt{kx}"])
                    for half in range(2):
                        po, pon = P(half + 2 * kx)
                        hs_ = slice(half * 512, (half + 1) * 512)
                        for kc in range(22):
                            S.mm(po, prv[k][:, kc, a * 128:(a + 1) * 128], wdv[:, kc, hs_], kc == 0, kc == 21,
                                 r=[f"pr{k}"] + wdn, w=[pon])
                        S.tt("dve", t1g[half], po, g2l[:, r, hs_], ALU.mult, r=[pon, "g2l"], w=[f"t1g{half}"])
                        S.tt("pool", x2[kx][:, hs_], t1g[half], xt[kx][:, hs_], ALU.add, r=[f"t1g{half}", f"xt{kx}"], w=[f"x2_{kx}"])
                    if last:
                        S.store(out_d[(tt - 2) * 128:(tt - 1) * 128, :], x2[kx], r=[f"x2_{kx}"], w=[f"out{tt}"])
                    else:
                        S.store(xres2[tt * 128:(tt + 1) * 128, :], x2[kx], r=[f"x2_{kx}"], w=[f"xres2_{tt}"])
            S.barrier()
        finals = []
        S.finalize(final_reads=finals)
    return nc, S


def host_consts():
    ident = np.eye(128, dtype=np.float32)
    s = np.arange(128)[:, None]; t = np.arange(128)[None, :]
    triU = (s <= t).astype(np.float32)
    triL = (s >= t).astype(np.float32)
    shsel = np.zeros((128, 64), np.float32)
    shsel[64 + np.arange(64), np.arange(64)] = 1.0
    rows = 4096 // 64
    row = np.repeat(np.arange(rows, dtype=np.float32), 64)
    col = np.tile(np.arange(64, dtype=np.float32), rows)
    n_freq = 16
    inv_freq = (np.float32(10000.0) ** (-np.arange(n_freq, dtype=np.float32) / np.float32(n_freq))).astype(np.float32)
    ang = np.concatenate([row[:, None] * inv_freq, col[:, None] * inv_freq], axis=-1).astype(np.float32)
    rope = np.zeros((NTOK, 64), np.float32)
    rope[:NCTX, 0:32] = 1.0
    rope[NCTX:, 0:32] = np.cos(ang)
    rope[NCTX:, 32:64] = np.sin(ang)
    return dict(ident=ident, triU=triU, triL=triL, shsel=shsel, rope=rope)


def host_inputs(inputs):
    f = lambda a: np.ascontiguousarray(np.asarray(a, dtype=np.float32))
    shared = {k: f(inputs[k]) for k in ("ada_w", "ada_b", "norm1_w", "norm2_w", "w_in", "w_conv_out", "mlstm_gate_b",
                                        "mlstm_norm_w", "w_mlstm_out", "q_norm_w", "k_norm_w", "w_attn_out", "w_out",
                                        "ffn_w_up", "ffn_w_down")}
    L = DEPTH
    shared["bgb"] = f(inputs["branch_gate_b"]).reshape(L, 24, 128).transpose(0, 2, 1).copy()
    shared["convw"] = f(inputs["conv_dw_w"]).reshape(L, 31, 4, 128).transpose(0, 3, 2, 1).reshape(L, 128, 124).copy()
    shared["convb"] = f(inputs["conv_dw_b"]).reshape(L, 4, 128).transpose(0, 2, 1).copy()
    shared["lnw"] = f(inputs["conv_ln_w"]).reshape(L, 4, 128).transpose(0, 2, 1).copy()
    shared["lnb"] = f(inputs["conv_ln_b"]).reshape(L, 4, 128).transpose(0, 2, 1).copy()
    shared["fcw"] = f(inputs["ffn_conv_w"]).reshape(L, 3, 44, 128).transpose(0, 3, 2, 1).reshape(L, 128, 132).copy()
    shared["fcb"] = f(inputs["ffn_conv_b"]).reshape(L, 44, 128).transpose(0, 2, 1).copy()
    shared.update(host_consts())
    x = f(inputs["x"]); c = f(inputs["c"]); ctx = f(inputs["ctx"]); c_ctx = f(inputs["c_ctx"])
    maps = []
    for b in range(x.shape[0]):
        m = dict(shared)
        m["xin"] = np.concatenate([ctx[b], x[b]], axis=0)
        cv = np.stack([c[b], c_ctx], axis=0)
        m["cT"] = cv.reshape(2, 8, 128).transpose(2, 0, 1).reshape(128, 16).copy()
        maps.append(m)
    return maps


def kernel(**inputs):
    maps = host_inputs(inputs)
    if "nc" not in _CACHE:
        _CACHE["nc"] = build()[0]
    res = run_bass_kernel_spmd(_CACHE["nc"], maps, core_ids=list(range(8)))
    return np.stack([np.asarray(r["out"], dtype=np.float32) for r in res.results], axis=0)
```

```python
import numpy as np
import ml_dtypes
import concourse.bass as bass
import concourse.mybir as mybir
from concourse.bass_utils import run_bass_kernel_spmd
from contextlib import ExitStack

F32 = mybir.dt.float32
BF16 = mybir.dt.bfloat16
ALU = mybir.AluOpType
AF = mybir.ActivationFunctionType
AX = mybir.AxisListType

D = 1024
NTOK = 4352
NT = 34
NCTX = 256
DEPTH = 2
NIN = 8976
DFF = 2816
EPS = 1e-6
COMPUTE = ("pe", "act", "dve", "pool")
SAME_ENG_SYNC = True
DEFER_STORES = True
SAME_ENG_RAW_ONLY = False
_CACHE = {}


class Op:
    __slots__ = ("id", "eng", "emit", "dma", "deps", "signal", "token")

    def __init__(self, id, eng, emit, dma):
        self.id = id; self.eng = eng; self.emit = emit; self.dma = dma
        self.deps = set(); self.signal = False; self.token = None


class Sched:
    def __init__(self, nc, es):
        self.nc = nc
        self.es = es
        self.ops = []
        self.eng_ops = {e: [] for e in ("pe", "act", "dve", "pool", "sp")}
        self.last_w = {}
        self.readers = {}
        self.n_dma_sems = {"sp": 40}
        self.dma_since_bar = []
        self.pending_stores = []

    def add(self, eng, emit, reads=(), writes=(), dma=False, extra_deps=()):
        op = Op(len(self.ops), eng, emit, dma)
        deps = set(extra_deps)
        same_ok = (not dma) and SAME_ENG_RAW_ONLY
        for b in reads:
            w = self.last_w.get(b)
            if w is not None:
                deps.add(w)
            if b.startswith("ps"):
                for r_ in self.readers.get(b, ()):
                    if self.ops[r_].eng != eng:
                        deps.add(r_)
        for b in writes:
            w = self.last_w.get(b)
            if w is not None and not (same_ok and self.ops[w].eng == eng and not self.ops[w].dma):
                deps.add(w)
            for r_ in self.readers.get(b, ()):
                if same_ok and self.ops[r_].eng == eng and not self.ops[r_].dma:
                    continue
                deps.add(r_)
        for b in reads:
            lst = self.readers.setdefault(b, [])
            if not dma:
                lst[:] = [r for r in lst if self.ops[r].dma or self.ops[r].eng != eng]
            lst.append(op.id)
        for b in writes:
            self.last_w[b] = op.id
            self.readers[b] = []
        deps.discard(op.id)
        best = {}
        out = set()
        for d in deps:
            o = self.ops[d]
            if o.dma:
                out.add(d)
            else:
                if o.eng == eng and not dma and (eng == "pe" or not SAME_ENG_SYNC):
                    continue
                if o.eng not in best or best[o.eng] < d:
                    best[o.eng] = d
        out.update(best.values())
        op.deps = out
        self.ops.append(op)
        self.eng_ops[eng].append(op)
        if dma:
            self.dma_since_bar.append(op.id)
        return op

    def barrier(self):
        self.flush_stores()
        ids = list(self.dma_since_bar)
        for e in COMPUTE:
            if self.eng_ops[e]:
                for o in reversed(self.eng_ops[e]):
                    if o.emit is not None and not o.dma:
                        ids.append(o.id)
                        break
        self.dma_since_bar = []
        for e in ("pe", "act", "dve", "pool", "sp"):
            self.add(e, None, extra_deps=ids)
        self.last_w = {}
        self.readers = {}

    def dma(self, out, in_, r=(), w=(), q="sp"):
        op = self.add(q, lambda e: e.dma_start(out=out, in_=in_), r, w, dma=True)
        if DEFER_STORES and self.pending_stores:
            self.flush_stores()
        return op

    def store(self, out, in_, r=(), w=()):
        if not DEFER_STORES:
            return self.dma(out, in_, r, w)
        if self.pending_stores:
            self.flush_stores()
        snap = {b: self.last_w.get(b) for b in r}
        self.pending_stores.append((out, in_, tuple(r), tuple(w), snap))

    def flush_stores(self):
        pend, self.pending_stores = self.pending_stores, []
        for (out, in_, r, w, snap) in pend:
            for b in r:
                assert self.last_w.get(b) == snap[b], ("deferred store source overwritten before flush", b)
            self.add("sp", lambda e, out=out, in_=in_: e.dma_start(out=out, in_=in_), r, w, dma=True)

    def mm(self, out, lhsT, rhs, start, stop, r, w):
        return self.add("pe", lambda e: e.matmul(out, lhsT=lhsT, rhs=rhs, start=start, stop=stop), r, w)

    def act(self, out, in_, func, r, w, bias=None, scale=None, accum=None):
        kw = {}
        if bias is not None:
            kw["bias"] = bias
        if scale is not None:
            kw["scale"] = scale
        if accum is not None:
            kw["accum_out"] = accum
        return self.add("act", lambda e: e.activation(out, in_, func, **kw), r, w)

    def tt(self, eng, out, in0, in1, op, r, w):
        return self.add(eng, lambda e: e.tensor_tensor(out, in0, in1, op=op), r, w)

    def ts(self, eng, out, in0, s1, s2, op0, op1, r, w):
        if s2 is None:
            return self.add(eng, lambda e: e.tensor_scalar(out, in0, s1, None, op0=op0), r, w)
        return self.add(eng, lambda e: e.tensor_scalar(out, in0, s1, s2, op0=op0, op1=op1), r, w)

    def stt(self, eng, out, in0, scalar, in1, op0, op1, r, w):
        return self.add(eng, lambda e: e.scalar_tensor_tensor(out, in0, scalar, in1, op0=op0, op1=op1), r, w)

    def copy(self, eng, out, in_, r, w):
        if eng == "act":
            return self.add(eng, lambda e: e.copy(out, in_), r, w)
        return self.add(eng, lambda e: e.tensor_copy(out, in_), r, w)

    def memset(self, eng, ap, val, w):
        return self.add(eng, lambda e: e.memset(ap, val), (), w)

    def recip(self, out, in_, r, w):
        return self.add("dve", lambda e: e.reciprocal(out, in_), r, w)

    def reduce_sum(self, out, in_, r, w):
        return self.add("dve", lambda e: e.tensor_reduce(out, in_, axis=AX.X, op=ALU.add), r, w)

    def finalize(self, final_reads=()):
        nc = self.nc
        self.flush_stores()
        self.add("sp", None, reads=final_reads, writes=(), extra_deps=list(self.dma_since_bar))
        for op in self.ops:
            for d in op.deps:
                self.ops[d].signal = True
        sems = {}
        for e in COMPUTE:
            sems[e] = self.es.enter_context(nc.semaphore("sem_" + e))
        dsems = {}
        for q, n in self.n_dma_sems.items():
            dsems[q] = [self.es.enter_context(nc.semaphore(f"dsem_{q}_{i}")) for i in range(n)]
        cnt = {e: 0 for e in COMPUTE}
        dcnt = {q: [0] * n for q, n in self.n_dma_sems.items()}
        drr = {q: 0 for q in self.n_dma_sems}
        prev_on_sem = {}
        for e, lst in self.eng_ops.items():
            for op in lst:
                if op.dma:
                    j = drr[e]; drr[e] = (j + 1) % len(dsems[e])
                    prev = dcnt[e][j]
                    dcnt[e][j] += 16
                    op.token = (dsems[e][j], dcnt[e][j], ("d", e, j))
                    prev_on_sem[op.id] = (dsems[e][j], prev, ("d", e, j))
                elif op.signal:
                    cnt[e] += 1
                    op.token = (sems[e], cnt[e], ("c", e))
        self.max_counts = (dict(cnt), {q: max(v) for q, v in dcnt.items()})
        ops = self.ops

        def emit_engine(ename, eh):
            observed = {}
            for op in self.eng_ops[ename]:
                waits = []
                for d in sorted(op.deps):
                    waits.append(ops[d].token)
                if op.dma:
                    sem, val, key = prev_on_sem[op.id]
                    if val > 0:
                        waits.append((sem, val, key))
                for sem, val, key in waits:
                    if observed.get(key, 0) < val:
                        eh.wait_ge(sem, val)
                        observed[key] = val
                if op.emit is None:
                    continue
                ins = op.emit(eh)
                if op.dma:
                    ins.then_inc(op.token[0], 16)
                elif op.signal:
                    ins.then_inc(op.token[0], 1)

        with nc.Block() as block:
            @block.sync
            def _(e):
                emit_engine("sp", e)

            @block.tensor
            def _(e):
                emit_engine("pe", e)

            @block.scalar
            def _(e):
                emit_engine("act", e)

            @block.vector
            def _(e):
                emit_engine("dve", e)

            @block.gpsimd
            def _(e):
                emit_engine("pool", e)


class Arena:
    def __init__(self, t, nwords):
        self.t = t; self.n = nwords; self.off = 0; self.gen = 0

    def reset(self):
        self.off = 0; self.gen += 1

    def f32(self, n):
        assert self.off + n <= self.n, ("arena overflow", self.off + n, self.n)
        v = self.t[:, self.off:self.off + n]
        self.off += n
        return v

    def bf16(self, n):
        words = (n + 1) // 2
        assert self.off + words <= self.n, ("arena overflow", self.off + words, self.n)
        v = self.t[:, self.off:self.off + words].bitcast(BF16)
        self.off += words
        return v[:, 0:n]


def tokblocks():
    return [(0, 256)] + [(256 + 512 * i, 512) for i in range(8)]


PAD31 = 15


def padpos(tok, pad):
    return pad + tok if tok < NCTX else 2 * pad + tok


def build(debug=False, nlayers=DEPTH, stop_phase=99):
    nc = bass.Bass("TRN2", target_bir_lowering=False)
    kin = "ExternalInput"

    def din(name, shape, dt=F32):
        return nc.dram_tensor(name, list(shape), dt, kind=kin).ap()

    dbg_kind = "ExternalOutput" if debug else "Internal"

    def dscr(name, shape, dt=F32):
        return nc.dram_tensor(name, list(shape), dt, kind=dbg_kind).ap()

    xin = din("xin", [NTOK, D])
    cT = din("cT", [128, 16])
    ada_w = din("ada_w", [DEPTH, D, 6 * D])
    ada_b = din("ada_b", [DEPTH, 6 * D])
    norm1_w = din("norm1_w", [DEPTH, D])
    norm2_w = din("norm2_w", [DEPTH, D])
    w_in = din("w_in", [DEPTH, D, NIN])
    bgb = din("bgb", [DEPTH, 128, 24])
    convw = din("convw", [DEPTH, 128, 4 * 31])
    convb = din("convb", [DEPTH, 128, 4])
    lnw = din("lnw", [DEPTH, 128, 4])
    lnb = din("lnb", [DEPTH, 128, 4])
    w_conv_out = din("w_conv_out", [DEPTH, 512, D])
    gate_b = din("mlstm_gate_b", [DEPTH, 16])
    mnorm_w = din("mlstm_norm_w", [DEPTH, D])
    w_mlstm_out = din("w_mlstm_out", [DEPTH, D, D])
    qnw = din("q_norm_w", [DEPTH, 64])
    knw = din("k_norm_w", [DEPTH, 64])
    w_attn_out = din("w_attn_out", [DEPTH, 512, D])
    w_out = din("w_out", [DEPTH, D, D])
    w_up = din("ffn_w_up", [DEPTH, D, 2 * DFF])
    fcw = din("fcw", [DEPTH, 128, 44 * 3])
    fcb = din("fcb", [DEPTH, 128, 44])
    w_down = din("ffn_w_down", [DEPTH, DFF, D])
    ident_d = din("ident", [128, 128])
    triU_d = din("triU", [128, 128])
    triL_d = din("triL", [128, 128])
    shsel_d = din("shsel", [128, 64])
    rope_d = din("rope", [NTOK, 64])
    out_d = nc.dram_tensor("out", [4096, D], F32, kind="ExternalOutput").ap()

    xres1 = dscr("xres1", [NTOK, D])
    xres2 = dscr("xres2", [NTOK, D])
    convT = dscr("convT", [512, NTOK])
    mQT = dscr("mQT", [1024, NTOK], BF16)
    mKT = dscr("mKT", [1024, NTOK], BF16)
    mK = dscr("mK", [NTOK, 1024], BF16)
    mV = dscr("mV", [NTOK, 1024], BF16)
    mO = dscr("mO", [NTOK, 1024], BF16)
    aQT = dscr("aQT", [512, NTOK], BF16)
    aKT = dscr("aKT", [128, NTOK], BF16)
    aV = dscr("aV", [NTOK, 128], BF16)
    bgT = dscr("bgT", [3072, NTOK], BF16)
    hF = dscr("hF", [NTOK, 1024])
    hmT = dscr("hmT", [1024, NTOK], BF16)
    hcT = dscr("hcT", [512, NTOK], BF16)
    aoT = dscr("aoT", [512, NTOK], BF16)
    mergedT = dscr("mergedT", [1024, NTOK], BF16)
    prodT = dscr("prodT", [DFF, NTOK], BF16)
    dbg_hT = dscr("dbg_hT", [128, 8 * NTOK], BF16) if debug else None
    dbg_gt = dscr("dbg_gt", [128, NT * 16]) if debug else None
    dbg_gates = dscr("dbg_gates", [128, 8 * NT * 4]) if debug else None

    with ExitStack() as es:
        S = Sched(nc, es)
        sbt = lambda n, s, d=F32: es.enter_context(nc.sbuf_tensor("sb_" + n, s, d))
        hT_t = sbt("hT", [128, 8 * NTOK], BF16)
        hT = hT_t[:].rearrange("p (k t) -> p k t", k=8)
        modd = nc.dram_tensor("modd", [128, 2 * 6 * D], F32, kind=dbg_kind).ap()
        moddv = modd.rearrange("p (r c f) -> p r c f", r=2, c=6)
        ident = sbt("identf", [128, 128]); identb = sbt("identb", [128, 128], BF16)
        triU = sbt("triU", [128, 128]); triL = sbt("triL", [128, 128])
        ones = sbt("ones", [128, 128]); shsel = sbt("shsel", [128, 64])
        Gt_t = sbt("Gt", [128, NT * 16])
        Gt = Gt_t[:].rearrange("p (t g) -> p t g", g=16)
        ARN = 34000
        arena_t = sbt("arena", [128, ARN])
        A = Arena(arena_t, ARN)
        PS = [es.enter_context(nc.psum_tensor(f"ps{i}", [128, 512], F32)) for i in range(8)]

        def P(i):
            return PS[i][:], f"ps{i}"

        S.dma(ident[:], ident_d, w=["ident"])
        S.dma(triU[:], triU_d, w=["triU"])
        S.dma(triL[:], triL_d, w=["triL"])
        S.dma(shsel[:], shsel_d, w=["shsel"])
        S.memset("pool", ones[:], 1.0, w=["ones"])
        eps_t = sbt("eps_t", [128, 1])
        S.memset("pool", eps_t[:], EPS, w=["eps_t"])
        S.copy("dve", identb[:], ident[:], r=["ident"], w=["identb"])

        def tiny_rstd(ss_ap, n, cnt, tag, rdeps):
            S.act(ss_ap, ss_ap, AF.Sqrt, r=rdeps, w=[tag], scale=1.0 / cnt, bias=eps_t[:, 0:1])
            S.recip(ss_ap, ss_ap, r=[tag], w=[tag])

        def norm_tile(xt, xname, tt, modl, bufs, it):
            r = 1 if tt < 2 else 0
            junk, ss, t1, hx = bufs
            k = it % 2
            S.act(junk, xt, AF.Square, r=[xname], w=["nt_junk", f"nt_ss{k}"], accum=ss[k])
            tiny_rstd(ss[k], 1, D, f"nt_ss{k}", [f"nt_ss{k}"])
            S.stt("dve", t1[k], xt, ss[k], modl[:, r, 1, :], ALU.mult, ALU.mult, r=[xname, f"nt_ss{k}", "modl"], w=[f"nt_t1{k}"])
            S.tt("dve", hx[k], t1[k], modl[:, r, 0, :], ALU.add, r=[f"nt_t1{k}", "modl"], w=[f"nt_hx{k}"])
            for half in range(2):
                pa, pn = P(half + 2 * k)
                for q in range(4):
                    kc = half * 4 + q
                    S.mm(pa[:, q * 128:(q + 1) * 128], hx[k][:, kc * 128:(kc + 1) * 128], identb[:], True, True,
                         r=[f"nt_hx{k}", "identb"], w=[pn])
                dst = hT[:, half * 4:(half + 1) * 4, tt * 128:(tt + 1) * 128]
                src = pa.rearrange("p (q t) -> p q t", q=4)
                S.copy("act" if half == 0 else "dve", dst, src, r=[pn], w=[f"hT{tt}"])

        def norm_bufs():
            junk = A.f32(D)
            ss = [A.f32(1), A.f32(1)]
            t1 = [A.f32(D), A.f32(D)]
            hx = [A.bf16(D), A.bf16(D)]
            return junk, ss, t1, hx

        hT_all = [f"hT{tt}" for tt in range(NT)]

        def load_weight(dst_bf, src_ap, stage, sname, dname, eng, nk, ncols):
            S.dma(stage, src_ap, w=[sname])
            S.copy(eng, dst_bf, stage, r=[sname], w=[dname])

        for l in range(nlayers):
            xsrc = xin if l == 0 else xres2
            last = (l == DEPTH - 1)
            A.reset()
            mod_f = A.f32(2 * 6 * D)
            mod = mod_f.rearrange("p (r c f) -> p r c f", r=2, c=6)
            ct = A.f32(16); sc = A.f32(16); scb = A.f32(16 * 128)
            scbv = scb.rearrange("p (r k m) -> p r k m", r=2, k=8)
            S.dma(ct, cT, w=["ct"])
            S.act(sc, ct, AF.Silu, r=["ct"], w=["sc"])
            S.copy("dve", scbv, sc.rearrange("p (r k) -> p r k", r=2).unsqueeze(3).broadcast_to([128, 2, 8, 128]), r=["sc"], w=["scb"])
            aw = [A.f32(8 * 512), A.f32(8 * 512)]
            ab = [A.f32(512), A.f32(512)]
            for cb in range(12):
                k = cb % 2
                S.dma(aw[k].rearrange("p (k c) -> p k c", k=8),
                      ada_w[l, :, cb * 512:(cb + 1) * 512].rearrange("(k p) c -> p k c", p=128), w=[f"aw{k}"])
                S.dma(ab[k], ada_b[l, cb * 512:(cb + 1) * 512].partition_broadcast(128), w=[f"ab{k}"])
                for r in range(2):
                    pa, pn = P(r + 2 * k)
                    for kc in range(8):
                        S.mm(pa, scbv[:, r, kc, :], aw[k][:, kc * 512:(kc + 1) * 512], kc == 0, kc == 7,
                             r=["scb", f"aw{k}"], w=[pn])
                    ci, off = divmod(cb * 512, D)
                    S.tt("dve", mod[:, r, ci, off:off + 512], pa, ab[k], ALU.add, r=[pn, f"ab{k}"], w=["mods"])
            nwb = A.f32(D)
            for (ci, nw) in ((1, norm1_w), (4, norm2_w)):
                S.dma(nwb, nw[l, :].partition_broadcast(128), w=["nwb"])
                for r in range(2):
                    S.stt("dve", mod[:, r, ci, :], mod[:, r, ci, :], 1.0, nwb, ALU.add, ALU.mult, r=["mods", "nwb"], w=["mods"])
            S.dma(modd, mod_f, r=["mods"], w=["modd"])
            S.barrier()
            if stop_phase <= 0:
                break

            A.reset()
            nb = norm_bufs()
            xt = [A.f32(D), A.f32(D)]
            modl = A.f32(4 * D).rearrange("p (r c f) -> p r c f", r=2, c=2)
            S.dma(modl, moddv[:, :, 0:2, :], w=["modl"])
            for tt in range(NT):
                k = tt % 2
                S.dma(xt[k], xsrc[tt * 128:(tt + 1) * 128, :], w=[f"xt{k}"])
                norm_tile(xt[k], f"xt{k}", tt, modl, nb, tt)
            if debug and l == 0:
                S.dma(dbg_hT, hT_t[:], r=hT_all, w=["dbg_hT"])
            S.barrier()
            if stop_phase <= 1:
                break

            A.reset()
            stage = A.f32(8 * 512)
            wb = [A.bf16(8 * 512), A.bf16(8 * 512)]
            PADTOT = NTOK + 3 * PAD31
            glu = A.bf16(PADTOT)
            sg = [A.f32(512), A.f32(512)]
            cst = [A.f32(512), A.f32(512)]
            cw = A.f32(4 * 31); cbias = A.f32(4)
            Dg = A.bf16(124 * 128)
            Dgv = Dg.rearrange("p (i m) -> p i m", i=124)
            S.dma(cw, convw[l], w=["cw"])
            S.dma(cbias, convb[l], w=["cbias"])
            glu_names = [f"glu{bi}" for bi in range(9)]
            S.memset("pool", glu, 0.0, w=glu_names)
            for i_ in range(124):
                S.ts("dve", Dgv[:, i_, :], ident[:], cw[:, i_:i_ + 1], None, ALU.mult, None,
                     r=["ident", "cw"], w=[f"Dg{i_ // 31}"])
            wcnt = 0

            def wload(c0, ncols, extra=()):
                nonlocal wcnt
                k = wcnt % 2
                wcnt += 1
                ranges = [(c0, ncols)] + list(extra)
                tot = sum(n for _, n in ranges)
                st = stage[:, 0:8 * tot].rearrange("p (k c) -> p k c", k=8)
                dstv = wb[k][:, 0:8 * tot].rearrange("p (k c) -> p k c", k=8)
                off = 0
                for (cc, n) in ranges:
                    S.dma(st[:, :, off:off + n], w_in[l, :, cc:cc + n].rearrange("(k p) c -> p k c", p=128), w=["wstage"])
                    off += n
                S.copy("dve" if k == 0 else "act", dstv, st, r=["wstage"], w=[f"wb{k}"])
                return dstv, f"wb{k}"

            pscnt = 0

            def ftype_block(wv, wn, tb0, ntok, col0=0):
                nonlocal pscnt
                pa, pn = P(pscnt % 6)
                pscnt += 1
                for kc in range(8):
                    S.mm(pa[:, 0:ntok], wv[:, kc, col0:col0 + 128], hT[:, kc, tb0:tb0 + ntok], kc == 0, kc == 7,
                         r=[wn] + hT_all[tb0 // 128:(tb0 + ntok) // 128], w=[pn])
                return pa[:, 0:ntok], pn

            nxt = (wload(0, 128), wload(512, 128))
            ccnt_ = 0
            for j in range(4):
                (wa, wan), (wg, wgn) = nxt
                for bi, (tb0, ntok) in enumerate(tokblocks()):
                    pg, pgn = ftype_block(wg, wgn, tb0, ntok)
                    k = bi % 2
                    S.act(sg[k][:, 0:ntok], pg, AF.Sigmoid, r=[pgn], w=[f"sg{k}"])
                    pa_, pan = ftype_block(wa, wan, tb0, ntok)
                    c0 = padpos(tb0, PAD31)
                    S.tt("dve", glu[:, c0:c0 + ntok], pa_, sg[k][:, 0:ntok], ALU.mult, r=[pan, f"sg{k}"], w=[f"glu{bi}"])
                if j + 1 < 4:
                    nxt = (wload((j + 1) * 128, 128), wload(512 + (j + 1) * 128, 128))
                for bi, (tb0, ntok) in enumerate(tokblocks()):
                    o0 = tb0 if tb0 < NCTX else tb0 + PAD31
                    pc_, pcn_ = P(6 + (ccnt_ % 2))
                    kk = ccnt_ % 2
                    ccnt_ += 1
                    for tap in range(31):
                        S.mm(pc_[:, 0:ntok], Dgv[:, j * 31 + tap, :], glu[:, o0 + tap:o0 + tap + ntok], tap == 0, tap == 30,
                             r=glu_names + [f"Dg{j}"], w=[pcn_])
                    S.act(cst[kk][:, 0:ntok], pc_[:, 0:ntok], AF.Identity, r=[pcn_, "cbias"], w=[f"cst{kk}"], bias=cbias[:, j:j + 1])
                    S.store(convT[j * 128:(j + 1) * 128, tb0:tb0 + ntok], cst[kk][:, 0:ntok], r=[f"cst{kk}"], w=[f"convT{j}_{bi}"])
            S.barrier()
            if stop_phase <= 2:
                break

            A.reset()
            stage = A.f32(8 * 512)
            wb = [A.bf16(8 * 512), A.bf16(8 * 512)]
            stg = [A.bf16(NTOK), A.bf16(NTOK)]
            bgbias = A.f32(24)
            S.dma(bgbias, bgb[l], w=["bgbias"])
            wcnt = 0
            fcount = 0
            groups = []
            for i in range(2):
                groups.append((1024 + i * 512, "q", i))
            for i in range(2):
                groups.append((2048 + i * 512, "k", i))
            for i in range(6):
                groups.append((5904 + i * 512, "bg", i))
            loaded = wload(groups[0][0], 512)
            for gidx, (c0, kind, gi) in enumerate(groups):
                wv, wn = loaded
                if gidx + 1 < len(groups):
                    loaded = wload(groups[gidx + 1][0], 512)
                for sub in range(4):
                    fb = gi * 4 + sub
                    k = fcount % 2
                    fcount += 1
                    for bi, (tb0, ntok) in enumerate(tokblocks()):
                        pp, ppn = ftype_block(wv, wn, tb0, ntok, col0=sub * 128)
                        dst = stg[k][:, tb0:tb0 + ntok]
                        eng_alt = (bi % 2 == 0)
                        if kind == "q":
                            S.copy("act" if eng_alt else "dve", dst, pp, r=[ppn], w=[f"stg{k}"])
                        elif kind == "k":
                            if eng_alt:
                                S.act(dst, pp, AF.Identity, r=[ppn], w=[f"stg{k}"], scale=0.0625)
                            else:
                                S.ts("dve", dst, pp, 0.0625, None, ALU.mult, None, r=[ppn], w=[f"stg{k}"])
                        else:
                            S.act(dst, pp, AF.Sigmoid, r=[ppn, "bgbias"], w=[f"stg{k}"], bias=bgbias[:, fb:fb + 1])
                    dd = {"q": mQT, "k": mKT, "bg": bgT}[kind]
                    S.store(dd[fb * 128:(fb + 1) * 128, :], stg[k], r=[f"stg{k}"], w=[f"{kind}T{fb}"])
            S.barrier()
            if stop_phase <= 3:
                break

            A.reset()
            stage = A.f32(8 * 512)
            wb = [A.bf16(8 * 512), A.bf16(8 * 512)]
            ost = [A.bf16(512), A.bf16(512)]
            wcnt = 0
            tcount = 0

            def ttype_block(wv, wn, tt, ncols):
                nonlocal pscnt
                pa, pn = P(pscnt % 4)
                pscnt += 1
                for kc in range(8):
                    S.mm(pa[:, 0:ncols], hT[:, kc, tt * 128:(tt + 1) * 128], wv[:, kc, 0:ncols], kc == 0, kc == 7,
                         r=[wn, f"hT{tt}"], w=[pn])
                return pa[:, 0:ncols], pn

            tg = []
            for i in range(2):
                tg.append((2048 + i * 512, "mk", i))
            for i in range(2):
                tg.append((3072 + i * 512, "mv", i))
            for i in range(2):
                tg.append((4096 + i * 512, "mo", i))
            tg.append((5776, "av", 0))
            ncol_of = lambda kind_: 128 if kind_ == "av" else 512
            loaded = wload(tg[0][0], ncol_of(tg[0][1]))
            wq_pre = None
            for gidx, (c0, kind, gi) in enumerate(tg):
                ncol = ncol_of(kind)
                wv, wn = loaded
                if gidx + 1 < len(tg):
                    loaded = wload(tg[gidx + 1][0], ncol_of(tg[gidx + 1][1]))
                else:
                    wq_pre = wload(5136, 512)
                for tt in range(NT):
                    pp, ppn = ttype_block(wv, wn, tt, ncol)
                    k = tcount % 2
                    tcount += 1
                    if kind == "mk":
                        if tt % 2 == 0:
                            S.act(ost[k], pp, AF.Identity, r=[ppn], w=[f"ost{k}"], scale=0.0625)
                        else:
                            S.ts("dve", ost[k], pp, 0.0625, None, ALU.mult, None, r=[ppn], w=[f"ost{k}"])
                        dd = mK
                    elif kind == "av":
                        S.copy("act" if tt % 2 == 0 else "dve", ost[k][:, 0:128], pp, r=[ppn], w=[f"ost{k}"])
                        S.store(aV[tt * 128:(tt + 1) * 128, :], ost[k][:, 0:128], r=[f"ost{k}"], w=[f"aV{tt}"])
                        continue
                    elif kind == "mv":
                        S.copy("act" if tt % 2 == 0 else "dve", ost[k], pp, r=[ppn], w=[f"ost{k}"])
                        dd = mV
                    else:
                        S.act(ost[k], pp, AF.Sigmoid, r=[ppn], w=[f"ost{k}"])
                        dd = mO
                    S.store(dd[tt * 128:(tt + 1) * 128, gi * 512:(gi + 1) * 512], ost[k], r=[f"ost{k}"], w=[f"{kind}{tt}_{gi}"])

            CUT = _CACHE.get("cut", 99)
            qnb = A.f32(64); knb = A.f32(64); gbb = A.f32(16)
            S.dma(qnb, qnw[l, :].partition_broadcast(128), w=["qnb"])
            S.dma(knb, knw[l, :].partition_broadcast(128), w=["knb"])
            S.dma(gbb, gate_b[l, :].partition_broadcast(128), w=["gbb"])
            cs = [A.f32(64), A.f32(64)]
            sq = A.f32(512); ssq = [A.f32(8), A.f32(8)]
            qn = [A.f32(512), A.f32(512)]
            ta = A.f32(256); tb_ = A.f32(256); tc = A.f32(256); td = A.f32(256)
            qr = [A.bf16(512), A.bf16(512)]
            vst = [A.bf16(128), A.bf16(128)]
            qTs = A.bf16(4 * NTOK); kTs = A.bf16(NTOK)
            qTsv = qTs.rearrange("p (j t) -> p j t", j=4)

            def qknorm_rope(pp, ppn, nh, wbc, wbn, tt, k, tag):
                n = nh * 64
                v3 = lambda ap: ap[:, 0:n].rearrange("p (h d) -> p h d", h=nh)
                S.act(sq[:, 0:n], pp, AF.Square, r=[ppn], w=["sq"])
                S.reduce_sum(ssq[k][:, 0:nh], v3(sq), r=["sq"], w=[f"ssq{k}"])
                tiny_rstd(ssq[k][:, 0:nh], nh, 64, f"ssq{k}", [f"ssq{k}"])
                S.tt("dve", v3(qn[k]), v3(pp), ssq[k][:, 0:nh].unsqueeze(2).broadcast_to([128, nh, 64]), ALU.mult,
                     r=[ppn, f"ssq{k}"], w=[f"qn{k}"])
                S.tt("pool", v3(qn[k]), v3(qn[k]), wbc.unsqueeze(1).broadcast_to([128, nh, 64]), ALU.mult,
                     r=[f"qn{k}", wbn], w=[f"qn{k}"])
                x1 = v3(qn[k])[:, :, 0:32]; x2 = v3(qn[k])[:, :, 32:64]
                cc = cs[tt % 2][:, 0:32].unsqueeze(1).broadcast_to([128, nh, 32])
                sn = cs[tt % 2][:, 32:64].unsqueeze(1).broadcast_to([128, nh, 32])
                h3 = lambda ap: ap[:, 0:nh * 32].rearrange("p (h d) -> p h d", h=nh)
                csn = f"cs{tt % 2}"
                S.tt("dve", h3(ta), x1, cc, ALU.mult, r=[f"qn{k}", csn], w=["ta"])
                S.tt("pool", h3(tb_), x2, sn, ALU.mult, r=[f"qn{k}", csn], w=["tb"])
                S.tt("dve", h3(tc), x1, sn, ALU.mult, r=[f"qn{k}", csn], w=["tc"])
                S.tt("pool", h3(td), x2, cc, ALU.mult, r=[f"qn{k}", csn], w=["td"])
                o3 = v3(qr[k])
                S.tt("dve", o3[:, :, 0:32], h3(ta), h3(tb_), ALU.subtract, r=["ta", "tb"], w=[f"qr{k}a"])
                S.tt("pool", o3[:, :, 32:64], h3(tc), h3(td), ALU.add, r=["tc", "td"], w=[f"qr{k}b"])

            wq, wqn = wq_pre
            wr, wrn = wload(5120, 16, extra=[(5648, 256)])
            for tt in range(NT if CUT > 0 else 0):
                k = tt % 2
                S.dma(cs[k], rope_d[tt * 128:(tt + 1) * 128, :], w=[f"cs{k}"])
                pp, ppn = ttype_block(wq, wqn, tt, 512)
                if CUT > 1:
                    qknorm_rope(pp, ppn, 8, qnb, "qnb", tt, k, "q")
                for half in range(2 if CUT > 2 else 0):
                    pa, pn = P(4 + half)
                    for q in range(2):
                        j = half * 2 + q
                        S.mm(pa[:, q * 128:(q + 1) * 128], qr[k][:, j * 128:(j + 1) * 128], identb[:], True, True,
                             r=[f"qr{k}a", f"qr{k}b", "identb"], w=[pn])
                    S.copy("act", qTsv[:, half * 2:half * 2 + 2, tt * 128:(tt + 1) * 128],
                           pa[:, 0:256].rearrange("p (q t) -> p q t", q=2), r=[pn], w=["qTs"])
                if CUT <= 3:
                    continue
                pgt, pgn = P(7)
                for kc in range(8):
                    S.mm(pgt[:, 0:16], hT[:, kc, tt * 128:(tt + 1) * 128], wr[:, kc, 0:16], kc == 0, kc == 7, r=[wrn, f"hT{tt}"], w=[pgn])
                S.tt("dve", Gt[:, tt, :], pgt[:, 0:16], gbb, ALU.add, r=[pgn, "gbb"], w=["Gt"])
                pa, pn = P(pscnt % 4)
                pscnt += 1
                for kc in range(8):
                    S.mm(pa[:, 0:128], hT[:, kc, tt * 128:(tt + 1) * 128], wr[:, kc, 16:144], kc == 0, kc == 7, r=[wrn, f"hT{tt}"], w=[pn])
                if CUT <= 4:
                    continue
                qknorm_rope(pa[:, 0:128], pn, 2, knb, "knb", tt, k, "k")
                if CUT <= 4.2:
                    continue
                pk, pkn = P(6)
                S.mm(pk[:, 0:128], qr[k][:, 0:128], identb[:], True, True, r=[f"qr{k}a", f"qr{k}b", "identb"], w=[pkn])
                if CUT <= 4.4:
                    continue
                S.copy("act", kTs[:, tt * 128:(tt + 1) * 128], pk[:, 0:128], r=[pkn], w=["kTs"])
            for j in range(4):
                S.store(aQT[j * 128:(j + 1) * 128, :], qTsv[:, j, :], r=["qTs"], w=[f"aQT{j}"])
            if CUT > 4.6:
                S.store(aKT, kTs, r=["kTs"], w=["aKT"])
            if debug and l == 0:
                S.dma(dbg_gt, Gt_t[:], r=["Gt"], w=["dbg_gt"])
            S.barrier()
            if stop_phase <= 4:
                break

            A.reset()
            lnwt = A.f32(4); lnbt = A.f32(4)
            S.dma(lnwt, lnw[l], w=["lnwt"]); S.dma(lnbt, lnb[l], w=["lnbt"])
            cv = [A.f32(4 * 512), A.f32(4 * 512)]
            sqv = A.f32(4 * 512)
            mean = A.f32(512); var = A.f32(512); msq = A.f32(512)
            tmpn = [A.f32(512), A.f32(512)]
            hst = [A.bf16(512), A.bf16(512)]
            for bi, (tb0, ntok) in enumerate(tokblocks()):
                k = bi % 2
                cvv = cv[k].rearrange("p (j t) -> p j t", j=4)
                S.dma(cvv[:, :, 0:ntok], convT[:, tb0:tb0 + ntok].rearrange("(j p) t -> p j t", p=128), w=[f"cv{k}"])
                pa, pan = P(0 + 2 * k); pb, pbn = P(1 + 2 * k)
                for j in range(4):
                    S.mm(pa[:, 0:ntok], ones[:], cvv[:, j, 0:ntok], j == 0, j == 3, r=["ones", f"cv{k}"], w=[pan])
                sqq = sqv.rearrange("p (j t) -> p j t", j=4)
                S.act(sqq[:, :, 0:ntok], cvv[:, :, 0:ntok], AF.Square, r=[f"cv{k}"], w=["sqv"])
                for j in range(4):
                    S.mm(pb[:, 0:ntok], ones[:], sqq[:, j, 0:ntok], j == 0, j == 3, r=["ones", "sqv"], w=[pbn])
                S.ts("dve", mean[:, 0:ntok], pa[:, 0:ntok], 1.0 / 512, None, ALU.mult, None, r=[pan], w=["mean"])
                S.tt("dve", msq[:, 0:ntok], mean[:, 0:ntok], mean[:, 0:ntok], ALU.mult, r=["mean"], w=["msq"])
                S.stt("dve", var[:, 0:ntok], pb[:, 0:ntok], 1.0 / 512, msq[:, 0:ntok], ALU.mult, ALU.subtract, r=[pbn, "msq"], w=["var"])
                S.ts("dve", var[:, 0:ntok], var[:, 0:ntok], EPS, None, ALU.add, None, r=["var"], w=["var"])
                S.act(var[:, 0:ntok], var[:, 0:ntok], AF.Sqrt, r=["var"], w=["var"])
                S.recip(var[:, 0:ntok], var[:, 0:ntok], r=["var"], w=["var"])
                for j in range(4):
                    kk = j % 2
                    S.tt("dve", tmpn[kk][:, 0:ntok], cvv[:, j, 0:ntok], mean[:, 0:ntok], ALU.subtract, r=[f"cv{k}", "mean"], w=[f"tmpn{kk}"])
                    S.tt("pool", tmpn[kk][:, 0:ntok], tmpn[kk][:, 0:ntok], var[:, 0:ntok], ALU.mult, r=[f"tmpn{kk}", "var"], w=[f"tmpn{kk}"])
                    S.act(hst[kk][:, 0:ntok], tmpn[kk][:, 0:ntok], AF.Silu, r=[f"tmpn{kk}", "lnwt", "lnbt"], w=[f"hst{kk}"],
                          scale=lnwt[:, j:j + 1], bias=lnbt[:, j:j + 1])
                    S.store(hcT[j * 128:(j + 1) * 128, tb0:tb0 + ntok], hst[kk][:, 0:ntok], r=[f"hst{kk}"], w=[f"hcT{j}_{bi}"])
            S.barrier()
            if stop_phase <= 5:
                break

            A.reset()
            Gli = [Gt[:, :, 0:4], Gt[:, :, 8:12]]
            Gpf = [Gt[:, :, 4:8], Gt[:, :, 12:16]]
            WI = []; DEC = []; RR = []; WS = []
            NC4 = NT * 4
            for d in range(2):
                nlf = A.f32(NC4); cums = A.f32(NC4); d1 = A.f32(NC4); d2 = A.f32(NC4)
                wi = A.f32(NC4); dec = A.f32(NC4); rr = A.f32(NC4); ws = A.f32(NC4)
                v34 = lambda ap: ap.rearrange("p (t h) -> p t h", h=4)
                S.act(v34(nlf), Gpf[d], AF.Exp, r=["Gt"], w=[f"nlf{d}"], scale=-1.0)
                S.act(nlf, nlf, AF.Ln, r=[f"nlf{d}"], w=[f"nlf{d}"], bias=1.0)
                pa, pan = P(7); pb, pbn = P(6)
                S.mm(pa[:, 0:NC4], (triU if d == 0 else triL)[:], nlf, True, True, r=["triU", "triL", f"nlf{d}"], w=[pan])
                S.mm(pb[:, 0:NC4], ones[:], nlf, True, True, r=["ones", f"nlf{d}"], w=[pbn])
                S.copy("act", cums, pa[:, 0:NC4], r=[pan], w=[f"cums{d}"])
                S.act(wi, pa[:, 0:NC4], AF.Exp, r=[pan], w=[f"WI{d}"], scale=-1.0)
                S.act(dec, pb[:, 0:NC4], AF.Exp, r=[pbn], w=[f"DEC{d}"], scale=-1.0)
                S.tt("dve", d1, pb[:, 0:NC4], cums, ALU.subtract, r=[pbn, f"cums{d}"], w=[f"d1{d}"])
                S.act(rr, d1, AF.Exp, r=[f"d1{d}"], w=[f"R{d}"])
                S.tt("dve", v34(d2), Gli[d], v34(d1), ALU.subtract, r=["Gt", f"d1{d}"], w=[f"d2{d}"])
                S.act(ws, d2, AF.Exp, r=[f"d2{d}"], w=[f"WS{d}"])
                WI.append(wi); DEC.append(dec); RR.append(rr); WS.append(ws)
                if debug and l == 0:
                    for qi, (tl, tn) in enumerate(((wi, f"WI{d}"), (dec, f"DEC{d}"), (rr, f"R{d}"), (ws, f"WS{d}"))):
                        o_ = (d * 4 + qi) * NC4
                        S.dma(dbg_gates[:, o_:o_ + NC4], tl, r=[tn], w=[f"dbg_gates{d}{qi}"])
            masks = [triU, triL]
            Cs = [A.f32(2 * 257) for _ in range(4)]
            Cbs = [A.bf16(2 * 258) for _ in range(4)]
            Cv = [c.rearrange("p (c e) -> p c e", c=2) for c in Cs]
            Cbv = [c.rearrange("p (c e) -> p c e", c=2) for c in Cbs]
            nwbm = A.f32(D)
            S.dma(nwbm, mnorm_w[l, :].partition_broadcast(128), w=["nwbm"])
            QTb = [A.bf16(8 * 256) for _ in range(2)]
            KTb = [A.bf16(8 * 256) for _ in range(2)]
            Ktb = [A.bf16(2 * 1024) for _ in range(2)]
            V1b = [A.bf16(2 * 4 * 257) for _ in range(2)]
            hFb = [A.f32(2 * 1024) for _ in range(2)]
            sOb = [A.bf16(2 * 1024) for _ in range(2)]
            hmTsb = [A.bf16(8 * 256) for _ in range(2)]
            QTv = [q.rearrange("p (g t) -> p g t", g=8) for q in QTb]
            KTv = [q.rearrange("p (g t) -> p g t", g=8) for q in KTb]
            Ktv = [q.rearrange("p (a f) -> p a f", a=2) for q in Ktb]
            V1v = [q.rearrange("p (a h e) -> p a h e", a=2, h=4) for q in V1b]
            hFv = [q.rearrange("p (a f) -> p a f", a=2) for q in hFb]
            sOv = [q.rearrange("p (a f) -> p a f", a=2) for q in sOb]
            hmTsv = [q.rearrange("p (g t) -> p g t", g=8) for q in hmTsb]
            PTb = [A.bf16(128) for _ in range(4)]
            tmpb = [A.f32(257) for _ in range(4)]
            numb = [A.f32(257) for _ in range(4)]
            denb = [A.f32(2) for _ in range(4)]
            Kwb = [A.bf16(256) for _ in range(4)]
            hsb = [A.f32(256) for _ in range(4)]
            hnb = [A.f32(256) for _ in range(4)]
            hgb = [A.bf16(256) for _ in range(4)]
            ssb = [A.f32(1) for _ in range(4)]
            junk4 = A.f32(256)
            for k in range(2):
                S.memset("pool", V1b[k], 1.0, w=[f"V1{k}"])
            mQTv = mQT.rearrange("(g p) t -> p g t", p=128)
            mKTv = mKT.rearrange("(g p) t -> p g t", p=128)
            hmTv = hmT.rearrange("(g p) t -> p g t", p=128)
            spans_f = [(0, [0, 1])] + [(2 + 2 * i, [0, 1]) for i in range(16)]
            spans_b = [(0, [1, 0])] + [(2 + 2 * i, [1, 0]) for i in reversed(range(16))]
            scnt = 0
            deferred = []
            for d in range(2):
                for h in range(4):
                    S.memset("pool", Cs[h], 0.0, w=[f"C{h}"])
                    S.memset("pool", Cbs[h], 0.0, w=[f"Cb{h}"])
                for (st0, order) in (spans_f if d == 0 else spans_b):
                    k = scnt % 2
                    scnt += 1
                    t0 = st0 * 128
                    S.dma(QTv[k], mQTv[:, :, t0:t0 + 256], w=[f"QT{k}"])
                    S.dma(KTv[k], mKTv[:, :, t0:t0 + 256], w=[f"KT{k}"])
                    S.dma(Ktv[k], mK[t0:t0 + 256, :].rearrange("(a p) f -> p a f", p=128), w=[f"Kt{k}"])
                    for a_ in range(2):
                        S.dma(V1v[k][:, a_, :, 0:256], mV[t0 + a_ * 128:t0 + (a_ + 1) * 128, :].rearrange("p (h e) -> p h e", h=4), w=[f"V1{k}"])
                    if d == 1:
                        S.dma(hFv[k], hF[t0:t0 + 256, :].rearrange("(a p) f -> p a f", p=128), w=[f"hF{k}"])
                        S.dma(sOv[k], mO[t0:t0 + 256, :].rearrange("(a p) f -> p a f", p=128), w=[f"sO{k}"])
                    for a in order:
                        tt = st0 + a
                        tsl = slice(a * 128, (a + 1) * 128)
                        for pair in range(2):
                            hp = (2 * pair, 2 * pair + 1)
                            col = {h: tt * 4 + h for h in hp}
                            hsl = {h: slice(h * 256, (h + 1) * 256) for h in hp}
                            pst, pstn = P(0) if pair == 0 else P(6)
                            pqc = {hp[0]: P(1), hp[1]: P(2)}
                            psv = {hp[0]: P(3), hp[1]: P(4)}
                            pdc = (P(5), P(7))
                            for h in hp:
                                S.act(Kwb[h], Ktv[k][:, a, hsl[h]], AF.Identity, r=[f"Kt{k}", f"WS{d}"], w=[f"Kw{h}"],
                                      scale=WS[d][:, col[h]:col[h] + 1])
                            for i_, h in enumerate(hp):
                                for dc in range(2):
                                    S.mm(pst[:, i_ * 128:(i_ + 1) * 128], KTv[k][:, h * 2 + dc, tsl], QTv[k][:, h * 2 + dc, tsl], dc == 0, dc == 1,
                                         r=[f"KT{k}", f"QT{k}"], w=[pstn])
                            for h in hp:
                                pq, pqn = pqc[h]
                                for dc in range(2):
                                    S.mm(pq[:, 0:257], QTv[k][:, h * 2 + dc, tsl], Cbv[h][:, dc, 0:257], dc == 0, dc == 1,
                                         r=[f"QT{k}", f"Cb{h}"], w=[pqn])
                            for i_, h in enumerate(hp):
                                S.stt("dve", PTb[h], pst[:, i_ * 128:(i_ + 1) * 128], WS[d][:, col[h]:col[h] + 1], masks[d][:], ALU.mult, ALU.mult,
                                      r=[pstn, f"WS{d}", "triU", "triL"], w=[f"PT{h}"])
                            for h in hp:
                                pq, pqn = pqc[h]
                                S.act(tmpb[h], pq[:, 0:257], AF.Identity, r=[pqn, f"WI{d}"], w=[f"tmp{h}"], scale=WI[d][:, col[h]:col[h] + 1])
                            for i_, h in enumerate(hp):
                                ps_, psn_ = psv[h]
                                S.mm(ps_[:, 0:257], PTb[h], V1v[k][:, a, h, :], True, True, r=[f"PT{h}", f"V1{k}"], w=[psn_])
                            for f_ in deferred:
                                f_()
                            deferred.clear()
                            for i_, h in enumerate(hp):
                                for dc in range(2):
                                    pd_, pdn_ = pdc[dc]
                                    S.mm(pd_[:, 0:257], Kwb[h][:, dc * 128:(dc + 1) * 128], V1v[k][:, a, h, :], True, True,
                                         r=[f"Kw{h}", f"V1{k}"], w=[pdn_])
                                    S.stt("dve", Cv[h][:, dc, :], Cv[h][:, dc, :], DEC[d][:, col[h]:col[h] + 1], pd_[:, 0:257], ALU.mult, ALU.add,
                                          r=[f"C{h}", f"DEC{d}", pdn_], w=[f"C{h}"])
                                S.copy("act", Cbv[h][:, :, 0:257], Cv[h], r=[f"C{h}"], w=[f"Cb{h}"])
                                ps_, psn_ = psv[h]
                                S.stt("dve", numb[h], ps_[:, 0:257], RR[d][:, col[h]:col[h] + 1], tmpb[h], ALU.mult, ALU.add,
                                      r=[psn_, f"R{d}", f"tmp{h}"], w=[f"num{h}"])
                                S.stt("dve", denb[h][:, 1:2], numb[h][:, 256:257], -1.0, numb[h][:, 256:257], ALU.mult, ALU.max, r=[f"num{h}"], w=[f"den{h}"])
                                S.ts("dve", denb[h][:, 1:2], denb[h][:, 1:2], 1.0, None, ALU.max, None, r=[f"den{h}"], w=[f"den{h}"])
                                S.recip(denb[h][:, 1:2], denb[h][:, 1:2], r=[f"den{h}"], w=[f"den{h}"])

                                def readout(h=h, k=k, a=a, tsl=tsl, d=d, hs_=hsl[h]):
                                    rc = denb[h][:, 1:2]
                                    if d == 0:
                                        S.act(hFv[k][:, a, hs_], numb[h][:, 0:256], AF.Identity, r=[f"num{h}", f"den{h}"], w=[f"hF{k}"], scale=rc)
                                        return
                                    S.stt("dve", hsb[h], numb[h][:, 0:256], rc, hFv[k][:, a, hs_], ALU.mult, ALU.add,
                                          r=[f"num{h}", f"den{h}", f"hF{k}"], w=[f"hs{h}"])
                                    S.act(junk4, hsb[h], AF.Square, r=[f"hs{h}"], w=["junk4", f"ss4{h}"], accum=ssb[h])
                                    tiny_rstd(ssb[h], 1, 256, f"ss4{h}", [f"ss4{h}"])
                                    S.stt("dve", hnb[h], hsb[h], ssb[h], nwbm[:, hs_], ALU.mult, ALU.mult,
                                          r=[f"hs{h}", f"ss4{h}", "nwbm"], w=[f"hn{h}"])
                                    S.tt("pool", hgb[h], hnb[h], sOv[k][:, a, hs_], ALU.mult, r=[f"hn{h}", f"sO{k}"], w=[f"hg{h}"])
                                    pt_, ptn_ = P(0)
                                    for dc in range(2):
                                        S.mm(pt_[:, 256 + dc * 128:256 + (dc + 1) * 128], hgb[h][:, dc * 128:(dc + 1) * 128], identb[:], True, True,
                                             r=[f"hg{h}", "identb"], w=[ptn_])
                                    S.copy("dve", hmTsv[k][:, h * 2:h * 2 + 2, tsl], pt_[:, 256:512].rearrange("p (c t) -> p c t", c=2),
                                           r=[ptn_], w=[f"hmTs{k}"])

                                deferred.append(readout)
                    for f_ in deferred:
                        f_()
                    deferred.clear()
                    if d == 0:
                        S.store(hF[t0:t0 + 256, :].rearrange("(a p) f -> p a f", p=128), hFv[k], r=[f"hF{k}"], w=[f"hFd{st0}"])
                    else:
                        S.store(hmTv[:, :, t0:t0 + 256], hmTsv[k], r=[f"hmTs{k}"], w=[f"hmT{st0}"])
                if d == 0:
                    S.barrier()
            S.barrier()
            if stop_phase <= 6:
                break

            A.reset()
            KTa = A.bf16(2 * NTOK)
            KTav = KTa.rearrange("p (v t) -> p v t", v=2)
            V1a = A.bf16(NT * 2 * 128)
            V1av = V1a.rearrange("p (a v e) -> p a v e", a=NT, v=2)
            Rfull = A.f32(512)
            Rf2 = [A.f32(512), A.f32(512)]
            rb2 = [A.f32(512), A.f32(512)]
            QTq = [A.bf16(512), A.bf16(512)]
            Ptb = [A.bf16(512) for _ in range(8)]
            rbt = A.f32(512)
            aost = [A.bf16(512), A.bf16(512)]
            S.dma(KTav[0:64, :, :], aKT.rearrange("(v d) t -> d v t", d=64), w=["KTa"])
            S.dma(KTav[64:128, :, :], aKT.rearrange("(v d) t -> d v t", d=64), w=["KTa"])
            S.memset("pool", V1a, 1.0, w=["V1a"])
            for v_ in range(2):
                S.dma(V1av[:, :, v_, 0:64], aV[:, v_ * 64:(v_ + 1) * 64].rearrange("(a p) e -> p a e", p=128), w=["V1a"])
            S.memset("pool", Rfull, 0.0, w=["Rfull"])
            blocks = [(0, 256, [0, 1])] + [(256 + 512 * i, 512, list(range(NT))) for i in range(8)]
            if last:
                blocks = blocks[1:]
            bcnt = 0
            for hpair in range(4):
                kv = hpair // 2
                for (q0, nq, ktiles) in blocks:
                    k = bcnt % 2
                    bcnt += 1
                    S.dma(QTq[k][:, 0:nq], aQT[hpair * 128:(hpair + 1) * 128, q0:q0 + nq], w=[f"QTq{k}"])
                    pos = (P(6), P(7))
                    nkt = len(ktiles)
                    def emit_qk(i):
                        kt = ktiles[i]
                        for hd in range(2):
                            ps_, psn = P((i % 3) * 2 + hd)
                            psl = slice(hd * 64, (hd + 1) * 64)
                            S.mm(ps_[:, 0:nq], KTav[psl, kv, kt * 128:(kt + 1) * 128], QTq[k][psl, 0:nq], True, True,
                                 r=["KTa", f"QTq{k}"], w=[psn])
                        for hd in range(2):
                            ps_, psn = P((i % 3) * 2 + hd)
                            sl_ = (i % 4) * 2 + hd
                            S.act(Ptb[sl_][:, 0:nq], ps_[:, 0:nq], AF.Exp, r=[psn], w=[f"Pt{sl_}"], scale=0.125)

                    for i in range(min(3, nkt)):
                        emit_qk(i)
                    for i in range(nkt):
                        if i + 3 < nkt:
                            emit_qk(i + 3)
                        for hd in range(2):
                            sl_ = (i % 4) * 2 + hd
                            po, pon = pos[hd]
                            S.mm(po[:, 0:nq], V1av[:, ktiles[i], kv, :], Ptb[sl_][:, 0:nq], i == 0, i == nkt - 1,
                                 r=["V1a", f"Pt{sl_}"], w=[pon])
                    for hd in range(2):
                        h = hpair * 2 + hd
                        po, pon = pos[hd]
                        rf = Rf2[hd]; rb_ = rb2[hd]
                        S.recip(rf[64:128, 0:nq], po[64:128, 0:nq], r=[pon], w=[f"Rf{hd}"])
                        S.dma(rb_[0:64, 0:nq], rf[64:128, 0:nq], r=[f"Rf{hd}"], w=[f"rb{hd}"])
                        S.tt("dve", aost[hd][0:64, 0:nq], po[0:64, 0:nq], rb_[0:64, 0:nq], ALU.mult, r=[pon, f"rb{hd}"], w=[f"aost{hd}"])
                        S.store(aoT[h * 64:(h + 1) * 64, q0:q0 + nq], aost[hd][0:64, 0:nq], r=[f"aost{hd}"], w=[f"aoT{h}_{q0}"])
            S.barrier()
            if stop_phase <= 7:
                break

            A.reset()
            stage = A.f32(4096)
            stv = stage.rearrange("p (k f) -> p k f", k=4)
            wco = A.bf16(4 * D); wmo = A.bf16(8 * D); wao = A.bf16(4 * D)
            wcov = wco.rearrange("p (k f) -> p k f", k=4)
            wmov = wmo.rearrange("p (k f) -> p k f", k=8)
            waov = wao.rearrange("p (k f) -> p k f", k=4)
            pieces = [(w_conv_out, 0, wcov, 0, "wco"), (w_mlstm_out, 0, wmov, 0, "wmo"), (w_mlstm_out, 4, wmov, 4, "wmo"),
                      (w_attn_out, 0, waov, 0, "wao")]
            hcnt = 0
            for pi, (wsrc, k0, dstv, dk0, dn) in enumerate(pieces):
                for hh in range(2):
                    sh = hcnt % 2
                    hcnt += 1
                    S.dma(stv[:, sh * 2:sh * 2 + 2, :], wsrc[l].rearrange("(k p) f -> p k f", p=128)[:, k0 + hh * 2:k0 + hh * 2 + 2, :], w=[f"wstage{sh}"])
                    S.copy("dve" if sh == 0 else "act", dstv[:, dk0 + hh * 2:dk0 + hh * 2 + 2, :], stv[:, sh * 2:sh * 2 + 2, :], r=[f"wstage{sh}"],
                           w=[dn + str(dk0)] if hh == 1 else [dn + str(dk0) + "h"])
            wnames = ["wco0", "wmo0", "wmo4", "wao0", "wco0h", "wmo0h", "wmo4h", "wao0h"]
            opnd = [A.bf16(16 * 512), A.bf16(16 * 512)]
            opv = [o.rearrange("p (k t) -> p k t", k=16) for o in opnd]
            gtb = [A.bf16(3 * 512), A.bf16(3 * 512)]
            gtv = [g.rearrange("p (b t) -> p b t", b=3) for g in gtb]
            m1s = [A.f32(512), A.f32(512)]; m2s = [A.f32(512), A.f32(512)]; m3s = [A.f32(512), A.f32(512)]
            mst = [A.bf16(512), A.bf16(512)]
            bgTv = bgT.rearrange("(b f p) t -> p b f t", b=3, p=128)
            fcnt = 0
            for bi, (tb0, ntok) in enumerate(tokblocks()):
                if last and bi == 0:
                    continue
                k = bi % 2
                S.dma(opv[k][:, 0:4, 0:ntok], hcT[:, tb0:tb0 + ntok].rearrange("(k p) t -> p k t", p=128), w=[f"op{k}"])
                S.dma(opv[k][:, 4:12, 0:ntok], hmT[:, tb0:tb0 + ntok].rearrange("(k p) t -> p k t", p=128), w=[f"op{k}"])
                S.dma(opv[k][:, 12:16, 0:ntok], aoT[:, tb0:tb0 + ntok].rearrange("(k p) t -> p k t", p=128), w=[f"op{k}"])
                for fb in range(8):
                    k2 = fcnt % 2
                    fcnt += 1
                    fsl = slice(fb * 128, (fb + 1) * 128)
                    S.dma(gtv[k2][:, :, 0:ntok], bgTv[:, :, fb, tb0:tb0 + ntok], w=[f"gt{k2}"])
                    pc, pcn = P(3 * k2); pm, pmn = P(3 * k2 + 1); pa, pan = P(3 * k2 + 2)
                    for kc in range(4):
                        S.mm(pc[:, 0:ntok], wcov[:, kc, fsl], opv[k][:, kc, 0:ntok], kc == 0, kc == 3, r=wnames + [f"op{k}"], w=[pcn])
                    for kc in range(8):
                        S.mm(pm[:, 0:ntok], wmov[:, kc, fsl], opv[k][:, 4 + kc, 0:ntok], kc == 0, kc == 7, r=wnames + [f"op{k}"], w=[pmn])
                    for kc in range(4):
                        S.mm(pa[:, 0:ntok], waov[:, kc, fsl], opv[k][:, 12 + kc, 0:ntok], kc == 0, kc == 3, r=wnames + [f"op{k}"], w=[pan])
                    m1 = m1s[k2]; m2 = m2s[k2]; m3 = m3s[k2]
                    S.tt("dve", m1[:, 0:ntok], pc[:, 0:ntok], gtv[k2][:, 0, 0:ntok], ALU.mult, r=[pcn, f"gt{k2}"], w=[f"m1{k2}"])
                    S.tt("dve", m2[:, 0:ntok], pm[:, 0:ntok], gtv[k2][:, 1, 0:ntok], ALU.mult, r=[pmn, f"gt{k2}"], w=[f"m2{k2}"])
                    S.tt("dve", m1[:, 0:ntok], m1[:, 0:ntok], m2[:, 0:ntok], ALU.add, r=[f"m1{k2}", f"m2{k2}"], w=[f"m1{k2}"])
                    S.tt("dve", m3[:, 0:ntok], pa[:, 0:ntok], gtv[k2][:, 2, 0:ntok], ALU.mult, r=[pan, f"gt{k2}"], w=[f"m3{k2}"])
                    S.tt("dve", mst[k2][:, 0:ntok], m1[:, 0:ntok], m3[:, 0:ntok], ALU.add, r=[f"m1{k2}", f"m3{k2}"], w=[f"mst{k2}"])
                    S.store(mergedT[fsl, tb0:tb0 + ntok], mst[k2][:, 0:ntok], r=[f"mst{k2}"], w=[f"mgT{fb}_{bi}"])
            S.barrier()
            if stop_phase <= 8:
                break

            A.reset()
            stage = A.f32(4096)
            stv = stage.rearrange("p (k f) -> p k f", k=4)
            wo = A.bf16(8 * D)
            wov = wo.rearrange("p (k f) -> p k f", k=8)
            for pi in range(4):
                sh = pi % 2
                S.dma(stv[:, sh * 2:sh * 2 + 2, :], w_out[l].rearrange("(k p) f -> p k f", p=128)[:, pi * 2:pi * 2 + 2, :], w=[f"wstage{sh}"])
                S.copy("dve" if sh == 0 else "act", wov[:, pi * 2:pi * 2 + 2, :], stv[:, sh * 2:sh * 2 + 2, :], r=[f"wstage{sh}"], w=[f"wo{pi}"])
            mgb = [A.bf16(8 * 512), A.bf16(8 * 512)]
            mgv = [m.rearrange("p (k t) -> p k t", k=8) for m in mgb]
            modl3 = A.f32(6 * D).rearrange("p (r c f) -> p r c f", r=2, c=3)
            S.dma(modl3, moddv[:, :, 2:5, :], w=["modl"])
            xt = [A.f32(D), A.f32(D)]
            t1g = [A.f32(512), A.f32(512)]
            x1 = [A.f32(D), A.f32(D)]
            nb = norm_bufs()
            tcnt = 0
            for bi, (tb0, ntok) in enumerate(tokblocks()):
                if last and bi == 0:
                    continue
                k = bi % 2
                S.dma(mgv[k][:, :, 0:ntok], mergedT[:, tb0:tb0 + ntok].rearrange("(k p) t -> p k t", p=128), w=[f"mg{k}"])
                for a in range(ntok // 128):
                    tt = tb0 // 128 + a
                    r = 1 if tt < 2 else 0
                    kx = tcnt % 2
                    tcnt += 1
                    S.dma(xt[kx], xsrc[tt * 128:(tt + 1) * 128, :], w=[f"xt{kx}"])
                    for half in range(2):
                        po, pon = P(4 + half + 2 * kx)
                        hs_ = slice(half * 512, (half + 1) * 512)
                        for kc in range(8):
                            S.mm(po, mgv[k][:, kc, a * 128:(a + 1) * 128], wov[:, kc, hs_], kc == 0, kc == 7,
                                 r=[f"mg{k}", "wo0", "wo1", "wo2", "wo3"], w=[pon])
                        S.tt("dve", t1g[half], po, modl3[:, r, 0, hs_], ALU.mult, r=[pon, "modl"], w=[f"t1g{half}"])
                        S.tt("pool", x1[kx][:, hs_], t1g[half], xt[kx][:, hs_], ALU.add, r=[f"t1g{half}", f"xt{kx}"], w=[f"x1_{kx}"])
                    S.store(xres1[tt * 128:(tt + 1) * 128, :], x1[kx], r=[f"x1_{kx}"], w=[f"xres1_{tt}"])
                    norm_tile(x1[kx], f"x1_{kx}", tt, modl3[:, :, 1:3, :], nb, tcnt)
            S.barrier()
            if stop_phase <= 9:
                break

            A.reset()
            PT3 = NTOK + 3
            n3 = PT3 - 2
            stage = A.f32(8 * 256)
            stv8 = stage.rearrange("p (k c) -> p k c", k=8)
            wbu = [A.bf16(8 * 256), A.bf16(8 * 256)]
            wbuv = [w_.rearrange("p (k c) -> p k c", k=8) for w_ in wbu]
            uaB = [A.f32(PT3), A.f32(PT3)]; ugB = [A.f32(PT3), A.f32(PT3)]
            ca = A.f32(n3); cg = A.f32(n3)
            prod = A.bf16(n3)
            fwt = A.f32(132); fbt = A.f32(44)
            S.dma(fwt, fcw[l], w=["fwt"]); S.dma(fbt, fcb[l], w=["fbt"])
            uan = [[f"ua{k}_{bi}" for bi in range(9)] for k in range(2)]
            ugn = [[f"ug{k}_{bi}" for bi in range(9)] for k in range(2)]
            for k in range(2):
                S.memset("pool", uaB[k], 0.0, w=uan[k])
                S.memset("pool", ugB[k], 0.0, w=ugn[k])
            def ffn_wload(j):
                k = j % 2
                S.dma(stv8[:, :, 0:128], w_up[l, :, j * 128:(j + 1) * 128].rearrange("(k p) c -> p k c", p=128), w=["wstage"])
                S.dma(stv8[:, :, 128:256], w_up[l, :, DFF + j * 128:DFF + (j + 1) * 128].rearrange("(k p) c -> p k c", p=128), w=["wstage"])
                S.copy("act", wbuv[k], stv8, r=["wstage"], w=[f"wbu{k}"])

            ffn_wload(0)
            for j in range(22):
                k = j % 2
                ua = uaB[k]; ug = ugB[k]
                for bi, (tb0, ntok) in enumerate(tokblocks()):
                    if last and bi == 0:
                        continue
                    pa, pan = P((bi % 2) * 2); pg, pgn = P((bi % 2) * 2 + 1)
                    hr = hT_all[tb0 // 128:(tb0 + ntok) // 128]
                    for kc in range(8):
                        S.mm(pa[:, 0:ntok], wbuv[k][:, kc, 0:128], hT[:, kc, tb0:tb0 + ntok], kc == 0, kc == 7, r=[f"wbu{k}"] + hr, w=[pan])
                    for kc in range(8):
                        S.mm(pg[:, 0:ntok], wbuv[k][:, kc, 128:256], hT[:, kc, tb0:tb0 + ntok], kc == 0, kc == 7, r=[f"wbu{k}"] + hr, w=[pgn])
                    c0 = padpos(tb0, 1)
                    S.copy("act", ua[:, c0:c0 + ntok], pa[:, 0:ntok], r=[pan], w=[uan[k][bi]])
                    S.copy("act", ug[:, c0:c0 + ntok], pg[:, 0:ntok], r=[pgn], w=[ugn[k][bi]])
                if j + 1 < 22:
                    ffn_wload(j + 1)
                wa_ = lambda tap: fwt[:, j * 3 + tap:j * 3 + tap + 1]
                wg_ = lambda tap: fwt[:, (22 + j) * 3 + tap:(22 + j) * 3 + tap + 1]
                S.ts("dve", cg, ug[:, 1:1 + n3], wg_(1), fbt[:, 22 + j:23 + j], ALU.mult, ALU.add, r=ugn[k] + ["fwt", "fbt"], w=["cg"])
                S.stt("dve", cg, ug[:, 0:n3], wg_(0), cg, ALU.mult, ALU.add, r=ugn[k] + ["fwt", "cg"], w=["cg"])
                S.stt("dve", cg, ug[:, 2:2 + n3], wg_(2), cg, ALU.mult, ALU.add, r=ugn[k] + ["fwt", "cg"], w=["cg"])
                sgl = ug[:, 1:1 + n3]
                S.act(sgl, cg, AF.Silu, r=["cg"] + ugn[k], w=ugn[k])
                S.ts("dve", ca, ua[:, 1:1 + n3], wa_(1), fbt[:, j:j + 1], ALU.mult, ALU.add, r=uan[k] + ["fwt", "fbt"], w=["ca"])
                S.stt("dve", ca, ua[:, 0:n3], wa_(0), ca, ALU.mult, ALU.add, r=uan[k] + ["fwt", "ca"], w=["ca"])
                S.stt("dve", ca, ua[:, 2:2 + n3], wa_(2), ca, ALU.mult, ALU.add, r=uan[k] + ["fwt", "ca"], w=["ca"])
                S.flush_stores()
                S.tt("dve", prod, ca, sgl, ALU.mult, r=["ca"] + ugn[k], w=["prod"])
                S.memset("pool", ug[:, NCTX + 1:NCTX + 2], 0.0, w=ugn[k])
                S.store(prodT[j * 128:(j + 1) * 128, 0:NCTX], prod[:, 0:NCTX], r=["prod"], w=[f"prodT{j}"])
                S.store(prodT[j * 128:(j + 1) * 128, NCTX:NTOK], prod[:, NCTX + 1:NTOK + 1], r=["prod"], w=[f"prodT{j}"])
            S.barrier()
            if stop_phase <= 10:
                break

            A.reset()
            stage = A.f32(4096)
            stv = stage.rearrange("p (k f) -> p k f", k=4)
            wd = A.bf16(22 * D)
            wdv = wd.rearrange("p (k f) -> p k f", k=22)
            wdn = []
            wsrc = w_down[l].rearrange("(k p) f -> p k f", p=128)
            for pi, k0 in enumerate(range(0, 22, 2)):
                sh = pi % 2
                S.dma(stv[:, sh * 2:sh * 2 + 2, :], wsrc[:, k0:k0 + 2, :], w=[f"wstage{sh}"])
                S.copy("dve" if sh == 0 else "act", wdv[:, k0:k0 + 2, :], stv[:, sh * 2:sh * 2 + 2, :], r=[f"wstage{sh}"], w=[f"wd{pi}"])
                wdn.append(f"wd{pi}")
            prb = [A.bf16(22 * 512), A.bf16(22 * 512)]
            prv = [p_.rearrange("p (k t) -> p k t", k=22) for p_ in prb]
            g2l = A.f32(2 * D).rearrange("p (r f) -> p r f", r=2)
            S.dma(g2l, moddv[:, :, 5, :], w=["g2l"])
            xt = [A.f32(D), A.f32(D)]
            t1g = [A.f32(512), A.f32(512)]
            x2 = [A.f32(D), A.f32(D)]
            xdst = out_d if last else xres2
            tcnt = 0
            for bi, (tb0, ntok) in enumerate(tokblocks()):
                if last and bi == 0:
                    continue
                k = bi % 2
                prsrc = prodT[:, tb0:tb0 + ntok].rearrange("(k p) t -> p k t", p=128)
                for k0 in range(0, 22, 6):
                    nk = min(6, 22 - k0)
                    S.dma(prv[k][:, k0:k0 + nk, 0:ntok], prsrc[:, k0:k0 + nk, :], w=[f"pr{k}"])
                for a in range(ntok // 128):
                    tt = tb0 // 128 + a
                    r = 1 if tt < 2 else 0
                    kx = tcnt % 2
                    tcnt += 1
                    S.dma(xt[kx], xres1[tt * 128:(tt + 1) * 128, :], w=[f"xt{kx}"])
                    for half in range(2):
                        po, pon = P(half + 2 * kx)
                        hs_ = slice(half * 512, (half + 1) * 512)
                        for kc in range(22):
                            S.mm(po, prv[k][:, kc, a * 128:(a + 1) * 128], wdv[:, kc, hs_], kc == 0, kc == 21,
                                 r=[f"pr{k}"] + wdn, w=[pon])
                        S.tt("dve", t1g[half], po, g2l[:, r, hs_], ALU.mult, r=[pon, "g2l"], w=[f"t1g{half}"])
                        S.tt("dve", x2[kx][:, hs_], t1g[half], xt[kx][:, hs_], ALU.add, r=[f"t1g{half}", f"xt{kx}"], w=[f"x2_{kx}"])
                    if last:
                        S.store(out_d[(tt - 2) * 128:(tt - 1) * 128, :], x2[kx], r=[f"x2_{kx}"], w=[f"out{tt}"])
                    else:
                        S.store(xres2[tt * 128:(tt + 1) * 128, :], x2[kx], r=[f"x2_{kx}"], w=[f"xres2_{tt}"])
            S.barrier()
        finals = []
        S.finalize(final_reads=finals)
    return nc, S


def host_consts():
    ident = np.eye(128, dtype=np.float32)
    s = np.arange(128)[:, None]; t = np.arange(128)[None, :]
    triU = (s <= t).astype(np.float32)
    triL = (s >= t).astype(np.float32)
    shsel = np.zeros((128, 64), np.float32)
    shsel[64 + np.arange(64), np.arange(64)] = 1.0
    rows = 4096 // 64
    row = np.repeat(np.arange(rows, dtype=np.float32), 64)
    col = np.tile(np.arange(64, dtype=np.float32), rows)
    n_freq = 16
    inv_freq = (np.float32(10000.0) ** (-np.arange(n_freq, dtype=np.float32) / np.float32(n_freq))).astype(np.float32)
    ang = np.concatenate([row[:, None] * inv_freq, col[:, None] * inv_freq], axis=-1).astype(np.float32)
    rope = np.zeros((NTOK, 64), np.float32)
    rope[:NCTX, 0:32] = 1.0
    rope[NCTX:, 0:32] = np.cos(ang)
    rope[NCTX:, 32:64] = np.sin(ang)
    return dict(ident=ident, triU=triU, triL=triL, shsel=shsel, rope=rope)


def host_inputs(inputs):
    f = lambda a: np.ascontiguousarray(np.asarray(a, dtype=np.float32))
    shared = {k: f(inputs[k]) for k in ("ada_w", "ada_b", "norm1_w", "norm2_w", "w_in", "w_conv_out", "mlstm_gate_b",
                                        "mlstm_norm_w", "w_mlstm_out", "q_norm_w", "k_norm_w", "w_attn_out", "w_out",
                                        "ffn_w_up", "ffn_w_down")}
    L = DEPTH
    shared["bgb"] = f(inputs["branch_gate_b"]).reshape(L, 24, 128).transpose(0, 2, 1).copy()
    shared["convw"] = f(inputs["conv_dw_w"]).reshape(L, 31, 4, 128).transpose(0, 3, 2, 1).reshape(L, 128, 124).copy()
    shared["convb"] = f(inputs["conv_dw_b"]).reshape(L, 4, 128).transpose(0, 2, 1).copy()
    shared["lnw"] = f(inputs["conv_ln_w"]).reshape(L, 4, 128).transpose(0, 2, 1).copy()
    shared["lnb"] = f(inputs["conv_ln_b"]).reshape(L, 4, 128).transpose(0, 2, 1).copy()
    shared["fcw"] = f(inputs["ffn_conv_w"]).reshape(L, 3, 44, 128).transpose(0, 3, 2, 1).reshape(L, 128, 132).copy()
    shared["fcb"] = f(inputs["ffn_conv_b"]).reshape(L, 44, 128).transpose(0, 2, 1).copy()
    shared.update(host_consts())
    x = f(inputs["x"]); c = f(inputs["c"]); ctx = f(inputs["ctx"]); c_ctx = f(inputs["c_ctx"])
    maps = []
    for b in range(x.shape[0]):
        m = dict(shared)
        m["xin"] = np.concatenate([ctx[b], x[b]], axis=0)
        cv = np.stack([c[b], c_ctx], axis=0)
        m["cT"] = cv.reshape(2, 8, 128).transpose(2, 0, 1).reshape(128, 16).copy()
        maps.append(m)
    return maps


def kernel(**inputs):
    maps = host_inputs(inputs)
    if "nc" not in _CACHE:
        _CACHE["nc"] = build()[0]
    res = run_bass_kernel_spmd(_CACHE["nc"], maps, core_ids=list(range(8)))
    return np.stack([np.asarray(r["out"], dtype=np.float32) for r in res.results], axis=0)
```
